# Optimizing a Trainium2 kernel written in Bass

```python
import jax, jax.numpy as jnp
from jax import lax
import numpy as np

D_MODEL = 2048
BATCH = 2
SEQ = 8192
DEPTH = 4

N_MIXERS = 2
FOX_HEADS = 16
FOX_HEAD_DIM = D_MODEL // FOX_HEADS
FOX_WIDTH = FOX_HEADS * FOX_HEAD_DIM
Q_BLOCK = 128
CONV_CHANNELS = D_MODEL
CONV_KERNEL = 31
RMS_EPS = 1e-6
LN_EPS = 1e-5
N_FOX = (DEPTH + 1) // 2
N_CONV = DEPTH // 2

kernel_name = 'hybrid_fox_conformer_conv_trunk'


def rmsnorm(x, g):
    xf = x.astype(jnp.float32)
    y = xf * lax.rsqrt(jnp.mean(xf * xf, axis=-1, keepdims=True) + RMS_EPS) * g.astype(jnp.float32)
    return y.astype(x.dtype)


def layernorm(x, g, b):
    xf = x.astype(jnp.float32)
    mu = jnp.mean(xf, axis=-1, keepdims=True)
    var = jnp.mean(jnp.square(xf - mu), axis=-1, keepdims=True)
    y = (xf - mu) * lax.rsqrt(var + LN_EPS) * g.astype(jnp.float32) + b.astype(jnp.float32)
    return y.astype(x.dtype)


def fox_mixer(h, w_in, b_f, w_out):
    B, S, _ = h.shape
    W, H, Dh = FOX_WIDTH, FOX_HEADS, FOX_HEAD_DIM
    proj = h @ w_in
    q, k, v, gate, f_logit = jnp.split(proj, [W, 2 * W, 3 * W, 4 * W], axis=-1)
    q = q.reshape(B, S, H, Dh).transpose(0, 2, 1, 3)
    kf = k.reshape(B, S, H, Dh).transpose(0, 2, 1, 3).astype(jnp.float32)
    v = v.reshape(B, S, H, Dh).transpose(0, 2, 1, 3)
    log_f = jax.nn.log_sigmoid((f_logit + b_f).astype(jnp.float32))
    c = jnp.cumsum(log_f, axis=1).transpose(0, 2, 1)
    nb = S // Q_BLOCK
    q_blocks = q.reshape(B, H, nb, Q_BLOCK, Dh).transpose(2, 0, 1, 3, 4)
    c_blocks = c.reshape(B, H, nb, Q_BLOCK).transpose(2, 0, 1, 3)
    starts = jnp.arange(nb, dtype=jnp.int32) * Q_BLOCK
    k_pos = jnp.arange(S, dtype=jnp.int32)
    scale = FOX_HEAD_DIM ** -0.5

    def attend(args):
        q_blk, c_blk, start = args
        s = jnp.einsum('bhqd,bhkd->bhqk', q_blk.astype(jnp.float32), kf) * scale
        s = s + c_blk[..., :, None] - c[:, :, None, :]
        q_pos = start + jnp.arange(Q_BLOCK, dtype=jnp.int32)
        s = jnp.where(q_pos[:, None] >= k_pos[None, :], s, -jnp.inf)
        p = jax.nn.softmax(s, axis=-1)
        return jnp.einsum('bhqk,bhkd->bhqd', p.astype(v.dtype), v)

    o = lax.map(attend, (q_blocks, c_blocks, starts))
    o = o.transpose(1, 0, 3, 2, 4).reshape(B, S, W)
    y = o * jax.nn.silu(gate)
    return y @ w_out


def conv_mixer(h, w_in, b_in, dw, dw_b, ln_g, ln_b, w_out):
    C = CONV_CHANNELS
    proj = h @ w_in + b_in
    a, b, gate = jnp.split(proj, [C, 2 * C], axis=-1)
    u = a * jax.nn.sigmoid(b)
    u = lax.conv_general_dilated(
        u, dw[:, None, :].astype(u.dtype), window_strides=(1,),
        padding=[(CONV_KERNEL - 1, 0)],
        dimension_numbers=('NWC', 'WIO', 'NWC'),
        feature_group_count=C) + dw_b
    u = jax.nn.silu(layernorm(u, ln_g, ln_b))
    y = u * jax.nn.silu(gate)
    return y @ w_out


def setup_inputs(seed: int = 0) -> dict:
    key = jax.random.key(seed)
    ks = jax.random.split(key, 16)
    D, W, H, C, K = D_MODEL, FOX_WIDTH, FOX_HEADS, CONV_CHANNELS, CONV_KERNEL
    nrm = jax.random.normal
    return {
        'x': nrm(ks[0], (BATCH, SEQ, D), jnp.float32),
        'norm_g': 1.0 + 0.02 * nrm(ks[1], (DEPTH, D), jnp.float32),
        'fox_w_in': nrm(ks[2], (N_FOX, D, 4 * W + H), jnp.float32) * D ** -0.5,
        'fox_b_f': 2.0 + 0.5 * nrm(ks[3], (N_FOX, H), jnp.float32),
        'fox_w_out': nrm(ks[4], (N_FOX, W, D), jnp.float32) * W ** -0.5,
        'conv_w_in': nrm(ks[5], (N_CONV, D, 3 * C), jnp.float32) * D ** -0.5,
        'conv_b_in': 0.02 * nrm(ks[6], (N_CONV, 3 * C), jnp.float32),
        'conv_dw': nrm(ks[7], (N_CONV, K, C), jnp.float32) * K ** -0.5,
        'conv_dw_b': 0.02 * nrm(ks[8], (N_CONV, C), jnp.float32),
        'conv_ln_g': 1.0 + 0.02 * nrm(ks[9], (N_CONV, C), jnp.float32),
        'conv_ln_b': 0.02 * nrm(ks[10], (N_CONV, C), jnp.float32),
        'conv_w_out': nrm(ks[11], (N_CONV, C, D), jnp.float32) * C ** -0.5,
        'final_norm_g': 1.0 + 0.02 * nrm(ks[12], (D,), jnp.float32),
    }


def reference(x, norm_g, fox_w_in, fox_b_f, fox_w_out, conv_w_in, conv_b_in, conv_dw,
              conv_dw_b, conv_ln_g, conv_ln_b, conv_w_out, final_norm_g):
    h = x
    for i in range(DEPTH):
        hn = rmsnorm(h, norm_g[i])
        j = i // N_MIXERS
        if i % N_MIXERS == 0:
            h = h + fox_mixer(hn, fox_w_in[j], fox_b_f[j], fox_w_out[j])
        else:
            h = h + conv_mixer(hn, conv_w_in[j], conv_b_in[j], conv_dw[j], conv_dw_b[j],
                               conv_ln_g[j], conv_ln_b[j], conv_w_out[j])
    return rmsnorm(h, final_norm_g)
```

```python
import contextlib
import numpy as np
import ml_dtypes
import concourse.bass as bass
import concourse.mybir as mybir
from concourse.bass_utils import run_bass_kernel_spmd

F32 = mybir.dt.float32
BF16 = mybir.dt.bfloat16
AF = mybir.ActivationFunctionType
ALU = mybir.AluOpType
AX = mybir.AxisListType

NCORES = 8
D = 2048
T = 2048
S = 8192
NCH = 16
RMS_EPS = 1e-6
LN_EPS = 1e-5
KCONV = 31
HALO = 32

_cache = {}
_dbg = None


def _run(nc, in_maps):
    res = run_bass_kernel_spmd(nc, in_maps, core_ids=list(range(NCORES)))
    return res.results


def build_proj(prologue, C, groups, outs, has_bias):
    nc = bass.Bass("TRN2", target_bir_lowering=False)
    es = contextlib.ExitStack()
    NG = len(groups)
    WB = 256
    nblk = (C + WB - 1) // WB
    dr = {}
    if prologue == "rmsnorm":
        dr["hT"] = nc.dram_tensor("hT", [D, T], F32, kind="ExternalInput").ap()
        dr["g"] = nc.dram_tensor("g", [128, NCH], F32, kind="ExternalInput").ap()
    elif prologue == "mul":
        dr["aT"] = nc.dram_tensor("aT", [D, T], F32, kind="ExternalInput").ap()
        dr["bT"] = nc.dram_tensor("bT", [D, T], F32, kind="ExternalInput").ap()
    else:
        dr["xT"] = nc.dram_tensor("xT", [D, T], BF16, kind="ExternalInput").ap()
    Wd = nc.dram_tensor("W", [D, C], F32, kind="ExternalInput").ap()
    Wv = Wd.rearrange("(c p) n -> p c n", p=128)
    if has_bias:
        biasd = nc.dram_tensor("bias", [128, NG], F32, kind="ExternalInput").ap()
    has_res = any(g["kind"] == "res" for g in groups)
    if has_res:
        rTd = nc.dram_tensor("rT", [C, T], F32, kind="ExternalInput").ap()
    od = {}
    for name, rows, dt in outs:
        od[name] = nc.dram_tensor(name, [rows, T], dt, kind="ExternalOutput").ap()

    sb = lambda n, s, d: es.enter_context(nc.sbuf_tensor(n, s, d))
    sem = lambda n: es.enter_context(nc.semaphore(n))
    with es:
        xn = sb("xn", [128, NCH, T], BF16)
        TT1 = 256
        n1 = T // TT1
        hbuf = [sb(f"hbuf{i}", [128, NCH, TT1], F32) for i in range(2)]
        if prologue == "rmsnorm":
            sq = sb("sq", [128, NCH, TT1], F32)
            rstd = [sb(f"rstd{i}", [128, TT1], F32) for i in range(2)]
            rtmp = sb("rtmp", [128, TT1], F32)
            gsb = sb("gsb", [128, NCH], F32)
            ones = sb("ones", [128, 128], F32)
        cst = sb("cst", [128, 2], F32)
        if prologue == "mul":
            bbuf = [sb(f"bbuf{i}", [128, NCH, TT1], F32) for i in range(2)]
        wst = [sb(f"wst{i}", [128, NCH, WB], F32) for i in range(2)]
        wbf = [sb(f"wbf{i}", [128, NCH, WB], BF16) for i in range(2)]
        obA = [sb(f"obA{i}", [128, 512], F32) for i in range(2)]
        obAh = [sb(f"obAh{i}", [128, 512], BF16) for i in range(2)]
        obD = [sb(f"obD{i}", [128, 512], F32) for i in range(2)]
        rbuf = [sb(f"rbuf{i}", [128, 512], F32) for i in range(2)]
        ltmp = [sb(f"ltmp{i}", [128, 512], F32) for i in range(2)]
        if has_bias:
            bsb = sb("bsb", [128, NG], F32)
        ps = es.enter_context(nc.psum_tensor("ps", [128, 8, 512], F32))
        NBK = 6
        STATB = 7

        s_c = sem("s_c")
        s_hld = [sem(f"s_hld{i}") for i in range(2)]
        s_sq = sem("s_sq")
        s_stat = sem("s_stat")
        s_rs = sem("s_rs")
        s_nrm = sem("s_nrm")
        s_wld = [sem(f"s_wld{i}") for i in range(2)]
        s_wc = sem("s_wc")
        s_psd = [sem(f"s_psd{i}") for i in range(NBK)]
        s_oA = sem("s_oA")
        s_oD = sem("s_oD")
        s_stA = [sem(f"s_stA{i}") for i in range(2)]
        s_stD = [sem(f"s_stD{i}") for i in range(2)]
        s_rld = [sem(f"s_rld{i}") for i in range(2)]
        s_as = sem("s_as")
        s_ms = sem("s_ms")
        s_rc = sem("s_rc")

        tiles = []
        cntA = cntD = 0
        for gi, g in enumerate(groups):
            for tt in range(T // 512):
                if g["kind"] == "res":
                    tiles.append([g["blk"], gi, tt, "D", cntD]); cntD += 1
                else:
                    tiles.append([g["blk"], gi, tt, "A", cntA]); cntA += 1
        ntile = len(tiles)
        blk_last = {}
        for i, tl in enumerate(tiles):
            blk_last[tl[0]] = i

        def eng_wait_prev_bank(eng, i):
            if i >= NBK:
                p = tiles[i - NBK]
                eng.wait_ge(s_oA if p[3] == "A" else s_oD, p[4] + 1)

        with nc.Block() as block:
            @block.sync
            def _(sp):
                cnt = 0
                if prologue == "rmsnorm":
                    sp.dma_start(out=gsb[:], in_=dr["g"][:, :]).then_inc(s_c, 16); cnt += 16
                if has_bias:
                    sp.dma_start(out=bsb[:], in_=biasd[:, :]).then_inc(s_c, 16); cnt += 16
                if prologue == "plain":
                    xv = dr["xT"].rearrange("(c p) t -> p c t", p=128)
                    for q in range(4):
                        sp.dma_start(out=xn[:, :, q * 512:(q + 1) * 512],
                                     in_=xv[:, :, q * 512:(q + 1) * 512]).then_inc(s_hld[0], 16)
                else:
                    src = dr["hT"] if prologue == "rmsnorm" else dr["aT"]
                    sv = src.rearrange("(c p) t -> p c t", p=128)
                    if prologue == "mul":
                        bv = dr["bT"].rearrange("(c p) t -> p c t", p=128)
                    for tt in range(n1):
                        if tt >= 2:
                            sp.wait_ge(s_nrm, tt - 1)
                        sp.dma_start(out=hbuf[tt % 2][:], in_=sv[:, :, tt * TT1:(tt + 1) * TT1]).then_inc(s_hld[tt % 2], 16)
                        if prologue == "mul":
                            sp.dma_start(out=bbuf[tt % 2][:], in_=bv[:, :, tt * TT1:(tt + 1) * TT1]).then_inc(s_hld[tt % 2], 16)
                def wload(b):
                    w = min(WB, C - b * WB)
                    sp.dma_start(out=wst[b % 2][:, :, 0:w], in_=Wv[:, :, b * WB:b * WB + w]).then_inc(s_wld[b % 2], 16)
                wload(0)
                if nblk > 1:
                    wload(1)
                def rload(j):
                    tl = [t_ for t_ in tiles if t_[3] == "D" and t_[4] == j][0]
                    g = groups[tl[1]]
                    if j >= 2:
                        sp.wait_ge(s_oD, j - 1)
                    sp.dma_start(out=rbuf[j % 2][:], in_=rTd[g["row"]:g["row"] + 128, tl[2] * 512:(tl[2] + 1) * 512]).then_inc(s_rld[j % 2], 16)
                if has_res:
                    rload(0)
                cur_blk = 0
                for i, (blk, gi, tt, eng, j) in enumerate(tiles):
                    g = groups[gi]
                    if eng == "D" and j + 1 < cntD:
                        rload(j + 1)
                    if eng == "A":
                        sp.wait_ge(s_oA, j + 1)
                        src_t = obAh[j % 2] if g["dt"] == BF16 else obA[j % 2]
                        stsem = s_stA[j % 2]
                    else:
                        sp.wait_ge(s_oD, j + 1)
                        src_t = obD[j % 2]
                        stsem = s_stD[j % 2]
                    m = g["m"]
                    sp.dma_start(out=od[g["out"]][g["row"]:g["row"] + m, tt * 512:(tt + 1) * 512],
                                 in_=src_t[0:m, :]).then_inc(stsem, 16)
                    if i == blk_last[blk] and blk + 2 < nblk:
                        sp.wait_ge(s_wc, blk + 1)
                        wload(blk + 2)
                for k in range(2):
                    nA = len([1 for jj in range(cntA) if jj % 2 == k])
                    nD = len([1 for jj in range(cntD) if jj % 2 == k])
                    if nA:
                        sp.wait_ge(s_stA[k], 16 * nA)
                    if nD:
                        sp.wait_ge(s_stD[k], 16 * nD)

            @block.gpsimd
            def _(pool):
                if prologue == "rmsnorm":
                    pool.memset(ones[:], 1.0).then_inc(s_ms, 1)
                pool.memset(cst[:, 0:1], RMS_EPS).then_inc(s_ms, 1)
                pool.memset(cst[:, 1:2], 1.0).then_inc(s_ms, 1)
                for b in range(nblk):
                    w = min(WB, C - b * WB)
                    pool.wait_ge(s_wld[b % 2], 16 * (b // 2 + 1))
                    if b >= 2:
                        il = blk_last[b - 2]
                        pool.wait_ge(s_psd[il % NBK], il // NBK + 1)
                    pool.tensor_copy(out=wbf[b % 2][:, :, 0:w], in_=wst[b % 2][:, :, 0:w]).then_inc(s_wc, 1)

            @block.scalar
            def _(act):
                if has_bias:
                    act.wait_ge(s_c, 16 * ((1 if prologue == "rmsnorm" else 0) + 1))
                act.wait_ge(s_ms, 3 if prologue == "rmsnorm" else 2)
                if prologue == "rmsnorm":
                    for tt in range(n1):
                        act.wait_ge(s_hld[tt % 2], 16 * (tt // 2 + 1))
                        if tt >= 1:
                            act.wait_ge(s_stat, tt)
                        act.activation(out=sq[:], in_=hbuf[tt % 2][:], func=AF.Square).then_inc(s_sq, 1)
                        act.wait_ge(s_stat, tt + 1)
                        if tt >= 1:
                            act.wait_ge(s_rc, tt)
                        act.activation(out=rtmp[:], in_=ps[:, STATB, 0:TT1], func=AF.Sqrt, scale=1.0 / D,
                                       bias=cst[:, 0:1]).then_inc(s_rs, 1)
                nas = 0
                for i, (blk, gi, tt, eng, j) in enumerate(tiles):
                    if eng != "A":
                        continue
                    g = groups[gi]
                    m = g["m"]
                    bank = i % NBK
                    act.wait_ge(s_psd[bank], i // NBK + 1)
                    if j >= 2:
                        act.wait_ge(s_stA[j % 2], 16 * (j // 2))
                    pin = ps[0:m, bank, :]
                    if g["kind"] == "act":
                        dst = obAh[j % 2] if g["dt"] == BF16 else obA[j % 2]
                        kw = {}
                        if g.get("bias") is not None:
                            kw["bias"] = bsb[0:m, g["bias"]:g["bias"] + 1]
                        act.activation(out=dst[0:m, :], in_=pin, func=g["func"], **kw).then_inc(s_oA, 1)
                    else:
                        lt = ltmp[j % 2]
                        act.activation(out=lt[0:m, :], in_=pin, func=AF.Exp, scale=-1.0,
                                       bias=bsb[0:m, g["bias"]:g["bias"] + 1]).then_inc(s_as, 1); nas += 1
                        act.wait_ge(s_as, nas)
                        act.activation(out=lt[0:m, :], in_=lt[0:m, :], func=AF.Ln, bias=cst[0:m, 1:2]).then_inc(s_as, 1); nas += 1
                        act.wait_ge(s_as, nas)
                        act.mul(out=obA[j % 2][0:m, :], in_=lt[0:m, :], mul=-1.0).then_inc(s_oA, 1)

            @block.tensor
            def _(pe):
                if prologue == "rmsnorm":
                    pe.wait_ge(s_ms, 1)
                    for tt in range(n1):
                        pe.wait_ge(s_sq, tt + 1)
                        if tt >= 1:
                            pe.wait_ge(s_rs, tt)
                        for c in range(NCH):
                            mm = pe.matmul(ps[:, STATB, 0:TT1], lhsT=ones[:], rhs=sq[:, c, :], start=(c == 0), stop=(c == NCH - 1))
                        mm.then_inc(s_stat, 1)
                    pe.wait_ge(s_nrm, n1)
                elif prologue == "mul":
                    pe.wait_ge(s_nrm, n1)
                else:
                    pe.wait_ge(s_hld[0], 16 * 4)
                cur = -1
                for i, (blk, gi, tt, eng, j) in enumerate(tiles):
                    g = groups[gi]
                    m = g["m"]
                    if blk != cur:
                        pe.wait_ge(s_wc, blk + 1)
                        cur = blk
                    eng_wait_prev_bank(pe, i)
                    bank = i % NBK
                    co = g["col"] - blk * WB
                    for c in range(NCH):
                        mm = pe.matmul(ps[0:m, bank, :], lhsT=wbf[blk % 2][:, c, co:co + m],
                                       rhs=xn[:, c, tt * 512:(tt + 1) * 512], start=(c == 0), stop=(c == NCH - 1))
                    mm.then_inc(s_psd[bank], 1)

            @block.vector
            def _(dve):
                if prologue == "rmsnorm":
                    dve.wait_ge(s_c, 32 if has_bias else 16)
                    for tt in range(n1):
                        dve.wait_ge(s_rs, tt + 1)
                        dve.reciprocal(out=rstd[tt % 2][:], in_=rtmp[:]).then_inc(s_rc, 1)
                        dve.wait_ge(s_rc, tt + 1)
                        for c in range(NCH):
                            ins = dve.scalar_tensor_tensor(out=xn[:, c, tt * TT1:(tt + 1) * TT1], in0=hbuf[tt % 2][:, c, :],
                                                           scalar=gsb[:, c:c + 1], in1=rstd[tt % 2][:],
                                                           op0=ALU.mult, op1=ALU.mult)
                        ins.then_inc(s_nrm, 1)
                elif prologue == "mul":
                    for tt in range(n1):
                        dve.wait_ge(s_hld[tt % 2], 32 * (tt // 2 + 1))
                        dve.tensor_tensor(out=xn[:, :, tt * TT1:(tt + 1) * TT1], in0=hbuf[tt % 2][:], in1=bbuf[tt % 2][:],
                                          op=ALU.mult).then_inc(s_nrm, 1)
                for i, (blk, gi, tt, eng, j) in enumerate(tiles):
                    if eng != "D":
                        continue
                    bank = i % NBK
                    dve.wait_ge(s_psd[bank], i // NBK + 1)
                    dve.wait_ge(s_rld[j % 2], 16 * (j // 2 + 1))
                    if j >= 2:
                        dve.wait_ge(s_stD[j % 2], 16 * (j // 2))
                    dve.tensor_tensor(out=obD[j % 2][:], in0=ps[:, bank, :], in1=rbuf[j % 2][:], op=ALU.add).then_inc(s_oD, 1)
    return nc


def build_attn():
    nc = bass.Bass("TRN2", target_bir_lowering=False)
    es = contextlib.ExitStack()
    NH = 4
    NQT = S // 512
    NKB = S // 128
    scale = 128.0 ** -0.5
    qTd = nc.dram_tensor("qT", [NH, 128, S], BF16, kind="ExternalInput").ap()
    kTd = nc.dram_tensor("kT", [NH, 128, S], BF16, kind="ExternalInput").ap()
    vd = nc.dram_tensor("v", [NH, S, 128], BF16, kind="ExternalInput").ap()
    lfd = nc.dram_tensor("lf", [NH, S], F32, kind="ExternalInput").ap()
    maskd = nc.dram_tensor("mask", [128, 128], BF16, kind="ExternalInput").ap()
    oTd = nc.dram_tensor("oT", [NH * 128, S], F32, kind="ExternalOutput").ap()
    cscr = nc.dram_tensor("cscr", [NH, S], F32).ap()
    hiscr = nc.dram_tensor("hiscr", [NH, S], BF16).ap()
    loscr = nc.dram_tensor("loscr", [NH, S], BF16).ap()

    sb = lambda n, s, d: es.enter_context(nc.sbuf_tensor(n, s, d))
    sem = lambda n: es.enter_context(nc.semaphore(n))
    with es:
        lf4 = sb("lf4", [NH, S], F32)
        c4 = sb("c4", [NH, S], F32)
        d4 = sb("d4", [NH, S], F32)
        hi4 = sb("hi4", [NH, S], BF16)
        lo4 = sb("lo4", [NH, S], BF16)
        kt_sb = sb("kt_sb", [128, S], BF16)
        qt_sb = sb("qt_sb", [128, S], BF16)
        v_sb = sb("v_sb", [128, NKB, 128], BF16)
        ck_all = sb("ck_all", [128, NH, NKB], F32)
        r_rep = sb("r_rep", [128, NH, NQT], F32)
        bK = sb("bK", [128, NQT, NKB], F32)
        dq = [sb(f"dq{i}", [2, 512], BF16) for i in range(2)]
        NP = 4
        pt = [sb(f"pt{i}", [128, 512], BF16) for i in range(NP)]
        acc = [sb(f"acc{i}", [128, 512], F32) for i in range(2)]
        rec = sb("rec", [128, 512], F32)
        ob = [sb(f"ob{i}", [128, 512], F32) for i in range(2)]
        ones2 = sb("ones2", [2, 128], BF16)
        onesf = sb("onesf", [128, 128], F32)
        mask = sb("mask_sb", [128, 128], BF16)
        ps = es.enter_context(nc.psum_tensor("ps", [128, 8, 512], F32))
        NS = 4
        OB0 = 4
        RB = 6

        s_mk = sem("s_mk")
        s_in = sem("s_in"); s_p0 = sem("s_p0"); s_scr = sem("s_scr"); s_tab = sem("s_tab")
        s_hd = sem("s_hd"); s_hl = sem("s_hl"); s_bk = sem("s_bk")
        s_dq = [sem(f"s_dq{i}") for i in range(2)]
        s_s = sem("s_s"); s_exp = sem("s_exp"); s_mask = sem("s_mask"); s_acc = sem("s_acc"); s_pv = sem("s_pv")
        s_rsum = sem("s_rsum"); s_ds = sem("s_ds"); s_on = sem("s_on")
        s_ost = [sem(f"s_ost{i}") for i in range(2)]
        s_ms = sem("s_ms")

        sched = []
        i = 0; md = 0
        for h in range(NH):
            hq = []
            for qt in range(NQT):
                tl = []
                for kb in range(4 * qt + 4):
                    dg = kb >= 4 * qt
                    col0 = (kb - 4 * qt) * 128 if dg else 0
                    tl.append((i, kb, col0, dg, md if dg else -1))
                    i += 1
                    if dg:
                        md += 1
                hq.append(tl)
            sched.append(hq)
        NT = i

        flat_q = [tl for hq in sched for tl in hq]

        def store(sp, Qs):
            h, qt = divmod(Qs, NQT)
            sp.wait_ge(s_on, Qs + 1)
            sp.dma_start(out=oTd[h * 128:(h + 1) * 128, qt * 512:(qt + 1) * 512], in_=ob[Qs % 2][:]).then_inc(s_ost[Qs % 2], 16)

        with nc.Block() as block:
            @block.sync
            def _(sp):
                sp.dma_start(out=lf4[:], in_=lfd[:, :]).then_inc(s_in, 16)
                sp.dma_start(out=mask[:], in_=maskd[:, :]).then_inc(s_mk, 16)
                sp.wait_ge(s_p0, 1)
                sp.dma_start(out=cscr[:, :], in_=c4[:]).then_inc(s_scr, 16)
                sp.wait_ge(s_p0, 2)
                sp.dma_start(out=hiscr[:, :], in_=hi4[:]).then_inc(s_scr, 16)
                sp.wait_ge(s_p0, 3)
                sp.dma_start(out=loscr[:, :], in_=lo4[:]).then_inc(s_scr, 16)
                sp.wait_ge(s_scr, 48)
                for h in range(NH):
                    sp.dma_start(out=ck_all[:, h, :], in_=cscr[h, :].rearrange("(n p) -> p n", p=128),
                                 allow_slow_non_contiguous=True).then_inc(s_tab, 16)
                    sp.dma_start(out=r_rep[:, h, :], in_=cscr[h, :].rearrange("(q w) -> q w", w=512)[:, 0].partition_broadcast(128),
                                 allow_slow_non_contiguous=True).then_inc(s_tab, 16)
                sp.wait_ge(s_scr, 48)
                Q = 0
                for h in range(NH):
                    if h >= 1:
                        sp.wait_ge(s_rsum, h * NQT)
                    sp.dma_start(out=kt_sb[:], in_=kTd[h, :, :]).then_inc(s_hl, 16)
                    sp.dma_start(out=qt_sb[:], in_=qTd[h, :, :]).then_inc(s_hl, 16)
                    sp.dma_start(out=v_sb[:], in_=vd[h, :, :].rearrange("(n p) d -> p n d", p=128)).then_inc(s_hl, 16)
                    for qt in range(NQT):
                        if Q >= 2:
                            last_i = flat_q[Q - 2][-1][0]
                            sp.wait_ge(s_s, last_i + 1)
                        sp.dma_start(out=dq[Q % 2][0:1, :], in_=hiscr[h:h + 1, qt * 512:(qt + 1) * 512]).then_inc(s_dq[Q % 2], 16)
                        sp.dma_start(out=dq[Q % 2][1:2, :], in_=loscr[h:h + 1, qt * 512:(qt + 1) * 512]).then_inc(s_dq[Q % 2], 16)
                        if Q >= 1:
                            store(sp, Q - 1)
                        Q += 1
                store(sp, Q - 1)
                for k in range(2):
                    sp.wait_ge(s_ost[k], 16 * (NH * NQT // 2))

            @block.gpsimd
            def _(pool):
                pool.memset(ones2[:], 1.0).then_inc(s_ms, 1)
                pool.memset(onesf[:], 1.0).then_inc(s_ms, 1)
                pool.wait_ge(s_mk, 16)
                for hq in sched:
                    for tl in hq:
                        for (i, kb, col0, dg, m) in tl:
                            if dg:
                                pool.wait_ge(s_exp, i + 1)
                                pool.tensor_tensor(out=pt[i % NP][:, col0:col0 + 128], in0=pt[i % NP][:, col0:col0 + 128],
                                                   in1=mask[:], op=ALU.mult).then_inc(s_mask, 1)

            @block.vector
            def _(dve):
                dve.wait_ge(s_in, 16)
                dve.tensor_tensor_scan(out=c4[:], data0=lf4[:], data1=lf4[:], initial=0.0, op0=ALU.add, op1=ALU.min).then_inc(s_p0, 1)
                dve.wait_ge(s_p0, 1)
                for qt in range(NQT):
                    sl = slice(qt * 512, (qt + 1) * 512)
                    ins = dve.tensor_scalar(out=d4[:, sl], in0=c4[:, sl], scalar1=c4[:, qt * 512:qt * 512 + 1], scalar2=1.0 / scale,
                                            op0=ALU.subtract, op1=ALU.mult)
                ins.then_inc(s_ds, 1)
                dve.wait_ge(s_ds, 1)
                dve.tensor_copy(out=hi4[:], in_=d4[:]).then_inc(s_p0, 1)
                dve.wait_ge(s_p0, 2)
                dve.tensor_copy(out=lf4[:], in_=hi4[:]).then_inc(s_ds, 1)
                dve.wait_ge(s_ds, 2)
                dve.tensor_tensor(out=d4[:], in0=d4[:], in1=lf4[:], op=ALU.subtract).then_inc(s_ds, 1)
                dve.wait_ge(s_ds, 3)
                dve.tensor_copy(out=lo4[:], in_=d4[:]).then_inc(s_p0, 1)
                dve.wait_ge(s_tab, 16 * 2 * NH)
                nds = 3
                Q = 0
                for h in range(NH):
                    if h >= 1:
                        dve.wait_ge(s_exp, flat_q[h * NQT - 1][-1][0] + 1)
                    for qt in range(NQT):
                        nkb = 4 * qt + 4
                        ins = dve.tensor_scalar(out=bK[:, qt, 0:nkb], in0=ck_all[:, h, 0:nkb], scalar1=-1.0,
                                                scalar2=r_rep[:, h, qt:qt + 1], op0=ALU.mult, op1=ALU.add)
                    ins.then_inc(s_bk, 1)
                    for qt in range(NQT):
                        tl = sched[h][qt]
                        a = acc[Q % 2]
                        for n, (i, kb, col0, dg, m) in enumerate(tl):
                            if dg:
                                dve.wait_ge(s_mask, m + 1)
                            else:
                                dve.wait_ge(s_exp, i + 1)
                            if n == 0:
                                if Q >= 2:
                                    dve.wait_ge(s_rsum, Q - 1)
                                dve.tensor_copy(out=a[:], in_=pt[i % NP][:]).then_inc(s_acc, 1)
                            else:
                                dve.wait_ge(s_acc, i)
                                dve.tensor_tensor(out=a[:, col0:512], in0=a[:, col0:512], in1=pt[i % NP][:, col0:512],
                                                  op=ALU.add).then_inc(s_acc, 1)
                        last_i = tl[-1][0]
                        dve.wait_ge(s_rsum, Q + 1)
                        dve.wait_ge(s_pv, last_i + 1)
                        dve.reciprocal(out=rec[:], in_=ps[:, RB, :]).then_inc(s_ds, 1); nds += 1
                        dve.wait_ge(s_ds, nds)
                        if Q >= 2:
                            dve.wait_ge(s_ost[Q % 2], 16 * (Q // 2))
                        dve.tensor_tensor(out=ob[Q % 2][:], in0=ps[:, OB0 + Q % 2, :], in1=rec[:], op=ALU.mult).then_inc(s_on, 1)
                        Q += 1

            @block.scalar
            def _(act):
                for h in range(NH):
                    act.wait_ge(s_bk, h + 1)
                    for qt in range(NQT):
                        for (i, kb, col0, dg, m) in sched[h][qt]:
                            act.wait_ge(s_s, i + 1)
                            if i >= NP:
                                act.wait_ge(s_pv, i - NP + 1)
                                act.wait_ge(s_acc, i - NP + 1)
                            act.activation(out=pt[i % NP][:, col0:512], in_=ps[:, i % NS, col0:512], func=AF.Exp,
                                           scale=scale, bias=bK[:, qt, kb:kb + 1]).then_inc(s_exp, 1)

            @block.tensor
            def _(pe):
                pe.wait_ge(s_ms, 2)
                Q = 0
                for h in range(NH):
                    pe.wait_ge(s_hl, 48 * (h + 1))
                    for qt in range(NQT):
                        tl = sched[h][qt]
                        pe.wait_ge(s_dq[Q % 2], 32 * (Q // 2 + 1))
                        if Q >= 2:
                            pe.wait_ge(s_on, Q - 1)
                        ob_bank = OB0 + Q % 2

                        def S_(t):
                            (i, kb, col0, dg, m) = t
                            if i >= NS:
                                pe.wait_ge(s_exp, i - NS + 1)
                            pe.matmul(ps[:, i % NS, col0:512], lhsT=kt_sb[:, kb * 128:(kb + 1) * 128],
                                      rhs=qt_sb[:, qt * 512 + col0:(qt + 1) * 512], start=True, stop=False)
                            pe.matmul(ps[:, i % NS, col0:512], lhsT=ones2[:], rhs=dq[Q % 2][:, col0:512],
                                      start=False, stop=True).then_inc(s_s, 1)

                        def PV_(t, n):
                            (i, kb, col0, dg, m) = t
                            if dg:
                                pe.wait_ge(s_mask, m + 1)
                            else:
                                pe.wait_ge(s_exp, i + 1)
                            pe.matmul(ps[:, ob_bank, col0:512], lhsT=v_sb[:, kb, :], rhs=pt[i % NP][:, col0:512],
                                      start=(n == 0), stop=(n == len(tl) - 1)).then_inc(s_pv, 1)

                        S_(tl[0])
                        for n, t in enumerate(tl):
                            if n + 1 < len(tl):
                                S_(tl[n + 1])
                            PV_(t, n)
                        pe.wait_ge(s_acc, tl[-1][0] + 1)
                        if Q >= 1:
                            pe.wait_ge(s_on, Q)
                        pe.matmul(ps[:, RB, :], lhsT=onesf[:], rhs=acc[Q % 2][:], start=True, stop=True).then_inc(s_rsum, 1)
                        Q += 1
                    pe.nop().then_inc(s_hd, 1)
    return nc


def build_conv():
    nc = bass.Bass("TRN2", target_bir_lowering=False)
    es = contextlib.ExitStack()
    TH = 1024
    NHALF = T // TH
    TE = TH + HALO
    KD = 16
    NADD = KCONV - KD - 1
    aTd = nc.dram_tensor("aT", [D, HALO + T], F32, kind="ExternalInput").ap()
    sbTd = nc.dram_tensor("sbT", [D, HALO + T], F32, kind="ExternalInput").ap()
    sgTd = nc.dram_tensor("sgT", [D, T], F32, kind="ExternalInput").ap()
    dwd = nc.dram_tensor("dw", [128, NCH, KCONV], F32, kind="ExternalInput").ap()
    dwbd = nc.dram_tensor("dwb", [128, NCH], F32, kind="ExternalInput").ap()
    lngd = nc.dram_tensor("lng", [128, NCH], F32, kind="ExternalInput").ap()
    lnbd = nc.dram_tensor("lnb", [128, NCH], F32, kind="ExternalInput").ap()
    yTd = nc.dram_tensor("yT", [D, T], BF16, kind="ExternalOutput").ap()
    aV = aTd.rearrange("(c p) t -> p c t", p=128)
    sbV = sbTd.rearrange("(c p) t -> p c t", p=128)
    sgV = sgTd.rearrange("(c p) t -> p c t", p=128)
    yV = yTd.rearrange("(c p) t -> p c t", p=128)

    sb = lambda n, s, d: es.enter_context(nc.sbuf_tensor(n, s, d))
    sem = lambda n: es.enter_context(nc.semaphore(n))
    with es:
        abuf = [sb(f"abuf{i}", [128, TE], F32) for i in range(2)]
        sbuf_ = [sb(f"sbuf{i}", [128, TE], F32) for i in range(2)]
        u = [sb(f"u{i}", [128, TE], F32) for i in range(2)]
        z_all = sb("z_all", [128, NCH, TH], F32)
        z2 = [sb(f"z2{i}", [128, TH], F32) for i in range(2)]
        tmp = [sb(f"tmp{i}", [128, TH], F32) for i in range(2)]
        zsq = sb("zsq", [128, TH], F32)
        mean = sb("mean", [128, TH], F32)
        msq = sb("msq", [128, TH], F32)
        rstd = sb("rstd", [128, TH], F32)
        sgb = [sb(f"sgb{i}", [128, TH], F32) for i in range(2)]
        sil = [sb(f"sil{i}", [128, TH], F32) for i in range(2)]
        yb = [sb(f"yb{i}", [128, TH], BF16) for i in range(2)]
        dw = sb("dw_sb", [128, NCH, KCONV], F32)
        dwb = sb("dwb_sb", [128, NCH], F32)
        lng = sb("lng_sb", [128, NCH], F32)
        lnb = sb("lnb_sb", [128, NCH], F32)
        onesf = sb("onesf", [128, 128], F32)
        cst = sb("cst", [128, 1], F32)
        ps = es.enter_context(nc.psum_tensor("ps", [128, 4, 512], F32))

        s_c = sem("s_c"); s_ms = sem("s_ms")
        s_ld = [sem(f"s_ld{i}") for i in range(2)]
        s_u = sem("s_u"); s_zd = sem("s_zd"); s_zp = sem("s_zp"); s_z = sem("s_z"); s_zsq = sem("s_zsq"); s_st = sem("s_st")
        s_dd = sem("s_dd"); s_pa = sem("s_pa"); s_t0 = sem("s_t0"); s_tm = sem("s_tm")
        s_stat = sem("s_stat"); s_sd = sem("s_sd"); s_n = sem("s_n"); s_sl = sem("s_sl"); s_y = sem("s_y")
        s_rcp = sem("s_rcp")
        s_sg = [sem(f"s_sg{i}") for i in range(2)]
        s_yst = [sem(f"s_yst{i}") for i in range(2)]

        NG = NHALF * NCH
        with nc.Block() as block:
            @block.sync
            def _(sp):
                sp.dma_start(out=dw[:], in_=dwd[:, :, :]).then_inc(s_c, 16)
                sp.dma_start(out=dwb[:], in_=dwbd[:, :]).then_inc(s_c, 16)
                sp.dma_start(out=lng[:], in_=lngd[:, :]).then_inc(s_c, 16)
                sp.dma_start(out=lnb[:], in_=lnbd[:, :]).then_inc(s_c, 16)

                def ld(G):
                    hf, c = divmod(G, NCH)
                    if G >= 2:
                        sp.wait_ge(s_u, G - 1)
                    sp.dma_start(out=abuf[G % 2][:], in_=aV[:, c, hf * TH:hf * TH + TE]).then_inc(s_ld[G % 2], 16)
                    sp.dma_start(out=sbuf_[G % 2][:], in_=sbV[:, c, hf * TH:hf * TH + TE]).then_inc(s_ld[G % 2], 16)

                def sgld(G):
                    hf, c = divmod(G, NCH)
                    if G >= 2:
                        sp.wait_ge(s_y, G - 1)
                    sp.dma_start(out=sgb[G % 2][:], in_=sgV[:, c, hf * TH:(hf + 1) * TH]).then_inc(s_sg[G % 2], 16)

                def yst(G):
                    hf, c = divmod(G, NCH)
                    sp.wait_ge(s_y, G + 1)
                    sp.dma_start(out=yV[:, c, hf * TH:(hf + 1) * TH], in_=yb[G % 2][:]).then_inc(s_yst[G % 2], 16)

                for hf in range(NHALF):
                    for c in range(NCH):
                        ld(hf * NCH + c)
                    for c in range(NCH):
                        G = hf * NCH + c
                        sgld(G)
                        if c >= 1:
                            yst(G - 1)
                    yst(hf * NCH + NCH - 1)
                for k in range(2):
                    sp.wait_ge(s_yst[k], 16 * (NG // 2))

            @block.gpsimd
            def _(pool):
                pool.memset(onesf[:], 1.0).then_inc(s_ms, 1)
                pool.memset(cst[:], LN_EPS).then_inc(s_ms, 1)
                pool.wait_ge(s_c, 64)

                def mk_u(G):
                    pool.wait_ge(s_ld[G % 2], 32 * (G // 2 + 1))
                    if G >= 2:
                        pool.wait_ge(s_zd, G - 1)
                        pool.wait_ge(s_t0, G - 1)
                        pool.wait_ge(s_tm, (G - 1) * NADD)
                    pool.tensor_tensor(out=u[G % 2][:], in0=abuf[G % 2][:], in1=sbuf_[G % 2][:], op=ALU.mult).then_inc(s_u, 1)

                for hf in range(NHALF):
                    mk_u(hf * NCH)
                    for c in range(NCH):
                        G = hf * NCH + c
                        if c + 1 < NCH:
                            mk_u(G + 1)
                        pool.wait_ge(s_t0, G + 1)
                        zz = z2[G % 2]
                        for n in range(NADD):
                            j = G * NADD + n
                            pool.wait_ge(s_tm, j + 1)
                            if j >= 1:
                                pool.wait_ge(s_pa, j)
                            pool.tensor_tensor(out=zz[:], in0=zz[:], in1=tmp[j % 2][:], op=ALU.add).then_inc(s_pa, 1)
                    for c in range(NCH):
                        G = hf * NCH + c
                        pool.wait_ge(s_sl, G + 1)
                        pool.wait_ge(s_sg[G % 2], 16 * (G // 2 + 1))
                        if G >= 2:
                            pool.wait_ge(s_yst[G % 2], 16 * (G // 2))
                        pool.tensor_tensor(out=yb[G % 2][:], in0=sil[G % 2][:], in1=sgb[G % 2][:], op=ALU.mult).then_inc(s_y, 1)

            @block.vector
            def _(dve):
                dve.wait_ge(s_c, 64)
                ndd = 0
                for hf in range(NHALF):
                    for c in range(NCH):
                        G = hf * NCH + c
                        dve.wait_ge(s_u, G + 1)
                        if hf >= 1:
                            dve.wait_ge(s_sl, (hf - 1) * NCH + c + 1)
                        zc = z_all[:, c, :]
                        for k in range(KD):
                            off = HALO - (KCONV - 1) + k
                            if k == 0:
                                ins = dve.tensor_scalar(out=zc, in0=u[G % 2][:, off:off + TH], scalar1=dw[:, c, 0:1],
                                                        scalar2=dwb[:, c:c + 1], op0=ALU.mult, op1=ALU.add)
                            else:
                                ins = dve.scalar_tensor_tensor(out=zc, in0=u[G % 2][:, off:off + TH], scalar=dw[:, c, k:k + 1],
                                                               in1=zc, op0=ALU.mult, op1=ALU.add)
                            if k < KD - 1:
                                ins.then_inc(s_dd, 1); ndd += 1
                                dve.wait_ge(s_dd, ndd)
                            else:
                                ins.then_inc(s_zd, 1)
                        dve.wait_ge(s_zd, G + 1)
                        dve.wait_ge(s_pa, (G + 1) * NADD)
                        dve.tensor_tensor(out=zc, in0=zc, in1=z2[G % 2][:], op=ALU.add).then_inc(s_z, 1)
                    dve.wait_ge(s_st, (hf + 1) * NCH)
                    if hf >= 1:
                        dve.wait_ge(s_n, hf * NCH)
                    psm = ps[:, 0:2, :].rearrange("p a b -> p (a b)")
                    psq = ps[:, 2:4, :].rearrange("p a b -> p (a b)")
                    dve.tensor_scalar(out=mean[:], in0=psm, scalar1=1.0 / D, scalar2=None, op0=ALU.mult).then_inc(s_dd, 1); ndd += 1
                    dve.wait_ge(s_dd, ndd)
                    dve.tensor_tensor(out=msq[:], in0=mean[:], in1=mean[:], op=ALU.mult).then_inc(s_dd, 1); ndd += 1
                    dve.wait_ge(s_dd, ndd)
                    dve.scalar_tensor_tensor(out=msq[:], in0=psq, scalar=1.0 / D, in1=msq[:], op0=ALU.mult, op1=ALU.subtract).then_inc(s_stat, 1)
                    dve.wait_ge(s_sd, hf + 1)
                    dve.reciprocal(out=rstd[:], in_=zsq[:]).then_inc(s_rcp, 1)
                    dve.wait_ge(s_rcp, hf + 1)
                    for c in range(NCH):
                        zc = z_all[:, c, :]
                        dve.tensor_tensor(out=zc, in0=zc, in1=mean[:], op=ALU.subtract).then_inc(s_dd, 1); ndd += 1
                        dve.wait_ge(s_dd, ndd)
                        dve.tensor_tensor(out=zc, in0=zc, in1=rstd[:], op=ALU.mult).then_inc(s_n, 1)

            @block.scalar
            def _(act):
                act.wait_ge(s_c, 64)
                act.wait_ge(s_ms, 2)
                def taps(G):
                    c = G % NCH
                    act.wait_ge(s_u, G + 1)
                    if G >= 2:
                        act.wait_ge(s_z, G - 1)
                    off = HALO - (KCONV - 1) + KD
                    act.activation(out=z2[G % 2][:], in_=u[G % 2][:, off:off + TH], func=AF.Copy,
                                   scale=dw[:, c, KD:KD + 1]).then_inc(s_t0, 1)
                    for n in range(NADD):
                        j = G * NADD + n
                        k = KD + 1 + n
                        off = HALO - (KCONV - 1) + k
                        if j >= 2:
                            act.wait_ge(s_pa, j - 1)
                        act.activation(out=tmp[j % 2][:], in_=u[G % 2][:, off:off + TH], func=AF.Copy,
                                       scale=dw[:, c, k:k + 1]).then_inc(s_tm, 1)

                for hf in range(NHALF):
                    taps(hf * NCH)
                    for c in range(NCH):
                        G = hf * NCH + c
                        if c + 1 < NCH:
                            taps(G + 1)
                        act.wait_ge(s_z, G + 1)
                        if G >= 1:
                            act.wait_ge(s_st, G)
                        if hf >= 1 and c == 0:
                            act.wait_ge(s_rcp, hf)
                        act.activation(out=zsq[:], in_=z_all[:, c, :], func=AF.Square).then_inc(s_zsq, 1)
                    act.wait_ge(s_stat, hf + 1)
                    act.wait_ge(s_st, (hf + 1) * NCH)
                    act.activation(out=zsq[:], in_=msq[:], func=AF.Sqrt, bias=cst[:, 0:1]).then_inc(s_sd, 1)
                    for c in range(NCH):
                        G = hf * NCH + c
                        act.wait_ge(s_n, G + 1)
                        if G >= 2:
                            act.wait_ge(s_y, G - 1)
                        act.activation(out=sil[G % 2][:], in_=z_all[:, c, :], func=AF.Silu, scale=lng[:, c:c + 1],
                                       bias=lnb[:, c:c + 1]).then_inc(s_sl, 1)

            @block.tensor
            def _(pe):
                pe.wait_ge(s_ms, 1)
                for hf in range(NHALF):
                    if hf >= 1:
                        pe.wait_ge(s_stat, hf)
                    for c in range(NCH):
                        G = hf * NCH + c
                        pe.wait_ge(s_z, G + 1)
                        for hh in range(2):
                            pe.matmul(ps[:, hh, :], lhsT=onesf[:], rhs=z_all[:, c, hh * 512:(hh + 1) * 512],
                                      start=(c == 0), stop=(c == NCH - 1))
                        pe.wait_ge(s_zsq, G + 1)
                        for hh in range(2):
                            mm = pe.matmul(ps[:, 2 + hh, :], lhsT=onesf[:], rhs=zsq[:, hh * 512:(hh + 1) * 512],
                                           start=(c == 0), stop=(c == NCH - 1))
                        mm.then_inc(s_st, 1)
    return nc


def build_norm():
    nc = bass.Bass("TRN2", target_bir_lowering=False)
    es = contextlib.ExitStack()
    TT1 = 256
    n1 = T // TT1
    hTd = nc.dram_tensor("hT", [D, T], F32, kind="ExternalInput").ap()
    gd = nc.dram_tensor("g", [128, NCH], F32, kind="ExternalInput").ap()
    oTd = nc.dram_tensor("outT", [D, T], F32, kind="ExternalOutput").ap()
    hV = hTd.rearrange("(c p) t -> p c t", p=128)
    oV = oTd.rearrange("(c p) t -> p c t", p=128)
    sb = lambda n, s, d: es.enter_context(nc.sbuf_tensor(n, s, d))
    sem = lambda n: es.enter_context(nc.semaphore(n))
    with es:
        hbuf = [sb(f"hbuf{i}", [128, NCH, TT1], F32) for i in range(2)]
        obuf = [sb(f"obuf{i}", [128, NCH, TT1], F32) for i in range(2)]
        sq = sb("sq", [128, NCH, TT1], F32)
        rtmp = sb("rtmp", [128, TT1], F32)
        rstd = sb("rstd", [128, TT1], F32)
        gsb = sb("gsb", [128, NCH], F32)
        ones = sb("ones", [128, 128], F32)
        cst = sb("cst", [128, 1], F32)
        ps = es.enter_context(nc.psum_tensor("ps", [128, 512], F32))
        s_c = sem("s_c"); s_ms = sem("s_ms"); s_sq = sem("s_sq"); s_stat = sem("s_stat"); s_rs = sem("s_rs")
        s_rc = sem("s_rc"); s_nrm = sem("s_nrm")
        s_hld = [sem(f"s_hld{i}") for i in range(2)]
        s_ost = [sem(f"s_ost{i}") for i in range(2)]
        with nc.Block() as block:
            @block.sync
            def _(sp):
                sp.dma_start(out=gsb[:], in_=gd[:, :]).then_inc(s_c, 16)
                for tt in range(n1):
                    if tt >= 2:
                        sp.wait_ge(s_nrm, tt - 1)
                    sp.dma_start(out=hbuf[tt % 2][:], in_=hV[:, :, tt * TT1:(tt + 1) * TT1]).then_inc(s_hld[tt % 2], 16)
                    if tt >= 1:
                        sp.wait_ge(s_nrm, tt)
                        sp.dma_start(out=oV[:, :, (tt - 1) * TT1:tt * TT1], in_=obuf[(tt - 1) % 2][:]).then_inc(s_ost[(tt - 1) % 2], 16)
                sp.wait_ge(s_nrm, n1)
                sp.dma_start(out=oV[:, :, (n1 - 1) * TT1:n1 * TT1], in_=obuf[(n1 - 1) % 2][:]).then_inc(s_ost[(n1 - 1) % 2], 16)
                for k in range(2):
                    sp.wait_ge(s_ost[k], 16 * (n1 // 2))

            @block.gpsimd
            def _(pool):
                pool.memset(ones[:], 1.0).then_inc(s_ms, 1)
                pool.memset(cst[:], RMS_EPS).then_inc(s_ms, 1)

            @block.scalar
            def _(act):
                act.wait_ge(s_ms, 2)
                for tt in range(n1):
                    act.wait_ge(s_hld[tt % 2], 16 * (tt // 2 + 1))
                    if tt >= 1:
                        act.wait_ge(s_stat, tt)
                    act.activation(out=sq[:], in_=hbuf[tt % 2][:], func=AF.Square).then_inc(s_sq, 1)
                    act.wait_ge(s_stat, tt + 1)
                    if tt >= 1:
                        act.wait_ge(s_rc, tt)
                    act.activation(out=rtmp[:], in_=ps[:, 0:TT1], func=AF.Sqrt, scale=1.0 / D, bias=cst[:, 0:1]).then_inc(s_rs, 1)

            @block.tensor
            def _(pe):
                pe.wait_ge(s_ms, 1)
                for tt in range(n1):
                    pe.wait_ge(s_sq, tt + 1)
                    if tt >= 1:
                        pe.wait_ge(s_rs, tt)
                    for c in range(NCH):
                        mm = pe.matmul(ps[:, 0:TT1], lhsT=ones[:], rhs=sq[:, c, :], start=(c == 0), stop=(c == NCH - 1))
                    mm.then_inc(s_stat, 1)

            @block.vector
            def _(dve):
                dve.wait_ge(s_c, 16)
                for tt in range(n1):
                    dve.wait_ge(s_rs, tt + 1)
                    if tt >= 1:
                        dve.wait_ge(s_nrm, tt)
                    dve.reciprocal(out=rstd[:], in_=rtmp[:]).then_inc(s_rc, 1)
                    dve.wait_ge(s_rc, tt + 1)
                    if tt >= 2:
                        dve.wait_ge(s_ost[tt % 2], 16 * (tt // 2))
                    for c in range(NCH):
                        ins = dve.scalar_tensor_tensor(out=obuf[tt % 2][:, c, :], in0=hbuf[tt % 2][:, c, :], scalar=gsb[:, c:c + 1],
                                                       in1=rstd[:], op0=ALU.mult, op1=ALU.mult)
                    ins.then_inc(s_nrm, 1)
    return nc


def _get(name, fn):
    if name not in _cache:
        _cache[name] = fn()
    return _cache[name]


def _fox_groups():
    gs = []
    names = ["qT", "kT", "vT", "sgT"]
    for gi in range(64):
        which = gi // 16
        gs.append(dict(kind="act", blk=(gi * 128) // 256, col=gi * 128, m=128, row=(gi % 16) * 128, out=names[which],
                       dt=(F32 if which == 3 else BF16), func=(AF.Silu if which == 3 else AF.Copy), bias=None))
    gs.append(dict(kind="lf", blk=32, col=8192, m=16, row=0, out="lfT", dt=F32, bias=64))
    outs = [("qT", D, BF16), ("kT", D, BF16), ("vT", D, BF16), ("sgT", D, F32), ("lfT", 16, F32)]
    return gs, outs


def _conv_groups():
    gs = []
    names = ["aT", "sbT", "sgT"]
    funcs = [AF.Identity, AF.Sigmoid, AF.Silu]
    for gi in range(48):
        which = gi // 16
        gs.append(dict(kind="act", blk=(gi * 128) // 256, col=gi * 128, m=128, row=(gi % 16) * 128, out=names[which],
                       dt=F32, func=funcs[which], bias=gi))
    outs = [("aT", D, F32), ("sbT", D, F32), ("sgT", D, F32)]
    return gs, outs


def _res_groups():
    gs = [dict(kind="res", blk=(gi * 128) // 256, col=gi * 128, m=128, row=gi * 128, out="hT_out", dt=F32) for gi in range(16)]
    return gs, [("hT_out", D, F32)]


def _pc(v):
    return np.ascontiguousarray(np.asarray(v, np.float32).reshape(NCH, 128).T)


def kernel(x, norm_g, fox_w_in, fox_b_f, fox_w_out, conv_w_in, conv_b_in, conv_dw,
           conv_dw_b, conv_ln_g, conv_ln_b, conv_w_out, final_norm_g):
    x = np.asarray(x, np.float32)
    B = x.shape[0]
    cores = [(b, t) for b in range(B) for t in range(4)]
    hT = [np.ascontiguousarray(x[b, t * T:(t + 1) * T, :].T) for (b, t) in cores]

    fg, fo = _fox_groups()
    cg, co = _conv_groups()
    rg, ro = _res_groups()
    k_fox_in = _get("fox_in", lambda: build_proj("rmsnorm", 8208, fg, fo, True))
    k_attn = _get("attn", build_attn)
    k_fox_out = _get("fox_out", lambda: build_proj("mul", D, rg, ro, False))
    k_conv_in = _get("conv_in", lambda: build_proj("rmsnorm", 3 * D, cg, co, True))
    k_conv = _get("conv", build_conv)
    k_conv_out = _get("conv_out", lambda: build_proj("plain", D, rg, ro, False))
    k_norm = _get("norm", build_norm)
    mask = np.triu(np.ones((128, 128), np.float32)).astype(ml_dtypes.bfloat16)

    for layer in range(4):
        j = layer // 2
        g2 = _pc(norm_g[layer])
        if layer % 2 == 0:
            W = np.ascontiguousarray(np.asarray(fox_w_in[j], np.float32))
            bias = np.zeros((128, 65), np.float32)
            bias[:16, 64] = -np.asarray(fox_b_f[j], np.float32)
            r1 = _run(k_fox_in, [{"hT": hT[c], "g": g2, "W": W, "bias": bias} for c in range(NCORES)])
            ims = []
            for (b, g) in cores:
                def cat(name, lo, hi):
                    return np.concatenate([np.asarray(r1[4 * b + t][name])[lo:hi, :] for t in range(4)], axis=1)
                qT = cat("qT", g * 512, (g + 1) * 512).reshape(4, 128, S)
                kT = cat("kT", g * 512, (g + 1) * 512).reshape(4, 128, S)
                vT = cat("vT", g * 512, (g + 1) * 512).reshape(4, 128, S)
                v = np.ascontiguousarray(vT.transpose(0, 2, 1))
                lf = np.ascontiguousarray(cat("lfT", 4 * g, 4 * g + 4))
                ims.append({"qT": np.ascontiguousarray(qT), "kT": np.ascontiguousarray(kT), "v": v, "lf": lf, "mask": mask})
            r2 = _run(k_attn, ims)
            Wo = np.ascontiguousarray(np.asarray(fox_w_out[j], np.float32))
            ims = []
            for ci, (b, t) in enumerate(cores):
                oT = np.concatenate([np.asarray(r2[4 * b + g]["oT"])[:, t * T:(t + 1) * T] for g in range(4)], axis=0)
                ims.append({"aT": np.ascontiguousarray(oT), "bT": np.asarray(r1[ci]["sgT"]), "W": Wo, "rT": hT[ci]})
            r3 = _run(k_fox_out, ims)
            hT = [np.asarray(r3[c]["hT_out"]) for c in range(NCORES)]
            if _dbg is not None:
                _dbg[f"o{layer}"] = [np.asarray(r2[c]["oT"]) for c in range(NCORES)]
        else:
            W = np.ascontiguousarray(np.asarray(conv_w_in[j], np.float32))
            bias = np.ascontiguousarray(np.asarray(conv_b_in[j], np.float32).reshape(48, 128).T)
            r1 = _run(k_conv_in, [{"hT": hT[c], "g": g2, "W": W, "bias": bias} for c in range(NCORES)])
            dw = np.ascontiguousarray(np.asarray(conv_dw[j], np.float32).T.reshape(NCH, 128, KCONV).transpose(1, 0, 2))
            ims = []
            for ci, (b, t) in enumerate(cores):
                def ext(name):
                    cur = np.asarray(r1[ci][name])
                    if t == 0:
                        halo = np.zeros((D, HALO), np.float32)
                    else:
                        halo = np.asarray(r1[ci - 1][name])[:, T - HALO:]
                    return np.ascontiguousarray(np.concatenate([halo, cur], axis=1))
                ims.append({"aT": ext("aT"), "sbT": ext("sbT"), "sgT": np.asarray(r1[ci]["sgT"]), "dw": dw,
                            "dwb": _pc(conv_dw_b[j]), "lng": _pc(conv_ln_g[j]), "lnb": _pc(conv_ln_b[j])})
            r2 = _run(k_conv, ims)
            Wo = np.ascontiguousarray(np.asarray(conv_w_out[j], np.float32))
            r3 = _run(k_conv_out, [{"xT": np.asarray(r2[c]["yT"]), "W": Wo, "rT": hT[c]} for c in range(NCORES)])
            hT = [np.asarray(r3[c]["hT_out"]) for c in range(NCORES)]
        if _dbg is not None:
            _dbg[f"h{layer}"] = [a.copy() for a in hT]
    r = _run(k_norm, [{"hT": hT[c], "g": _pc(final_norm_g)} for c in range(NCORES)])
    out = np.empty((B, S, D), np.float32)
    for ci, (b, t) in enumerate(cores):
        out[b, t * T:(t + 1) * T, :] = np.asarray(r[ci]["outT"]).T
    return out
```

```python
import contextlib
import numpy as np
import ml_dtypes
import concourse.bass as bass
import concourse.mybir as mybir
from concourse.bass_utils import run_bass_kernel_spmd

F32 = mybir.dt.float32
BF16 = mybir.dt.bfloat16
AF = mybir.ActivationFunctionType
ALU = mybir.AluOpType
AX = mybir.AxisListType

NCORES = 8
D = 2048
T = 2048
S = 8192
NCH = 16
RMS_EPS = 1e-6
LN_EPS = 1e-5
KCONV = 31
HALO = 32

_cache = {}
_dbg = None


def _run(nc, in_maps):
    res = run_bass_kernel_spmd(nc, in_maps, core_ids=list(range(NCORES)))
    return res.results


def build_proj(prologue, C, groups, outs, has_bias):
    nc = bass.Bass("TRN2", target_bir_lowering=False)
    es = contextlib.ExitStack()
    NG = len(groups)
    WB = 256
    nblk = (C + WB - 1) // WB
    dr = {}
    if prologue == "rmsnorm":
        dr["hT"] = nc.dram_tensor("hT", [D, T], F32, kind="ExternalInput").ap()
        dr["g"] = nc.dram_tensor("g", [128, NCH], F32, kind="ExternalInput").ap()
    elif prologue == "mul":
        dr["aT"] = nc.dram_tensor("aT", [D, T], F32, kind="ExternalInput").ap()
        dr["bT"] = nc.dram_tensor("bT", [D, T], F32, kind="ExternalInput").ap()
    else:
        dr["xT"] = nc.dram_tensor("xT", [D, T], BF16, kind="ExternalInput").ap()
    Wd = nc.dram_tensor("W", [D, C], F32, kind="ExternalInput").ap()
    Wv = Wd.rearrange("(c p) n -> p c n", p=128)
    if has_bias:
        biasd = nc.dram_tensor("bias", [128, NG], F32, kind="ExternalInput").ap()
    has_res = any(g["kind"] == "res" for g in groups)
    if has_res:
        rTd = nc.dram_tensor("rT", [C, T], F32, kind="ExternalInput").ap()
    od = {}
    for name, rows, dt in outs:
        od[name] = nc.dram_tensor(name, [rows, T], dt, kind="ExternalOutput").ap()

    sb = lambda n, s, d: es.enter_context(nc.sbuf_tensor(n, s, d))
    sem = lambda n: es.enter_context(nc.semaphore(n))
    with es:
        xn = sb("xn", [128, NCH, T], BF16)
        TT1 = 256
        n1 = T // TT1
        hbuf = [sb(f"hbuf{i}", [128, NCH, TT1], F32) for i in range(2)]
        if prologue == "rmsnorm":
            sq = sb("sq", [128, NCH, TT1], F32)
            rstd = [sb(f"rstd{i}", [128, TT1], F32) for i in range(2)]
            rtmp = sb("rtmp", [128, TT1], F32)
            gsb = sb("gsb", [128, NCH], F32)
            ones = sb("ones", [128, 128], F32)
        cst = sb("cst", [128, 2], F32)
        if prologue == "mul":
            bbuf = [sb(f"bbuf{i}", [128, NCH, TT1], F32) for i in range(2)]
        wst = [sb(f"wst{i}", [128, NCH, WB], F32) for i in range(2)]
        wbf = [sb(f"wbf{i}", [128, NCH, WB], BF16) for i in range(2)]
        obA = [sb(f"obA{i}", [128, 512], F32) for i in range(2)]
        obAh = [sb(f"obAh{i}", [128, 512], BF16) for i in range(2)]
        obD = [sb(f"obD{i}", [128, 512], F32) for i in range(2)]
        rbuf = [sb(f"rbuf{i}", [128, 512], F32) for i in range(2)]
        ltmp = [sb(f"ltmp{i}", [128, 512], F32) for i in range(2)]
        if has_bias:
            bsb = sb("bsb", [128, NG], F32)
        ps = es.enter_context(nc.psum_tensor("ps", [128, 8, 512], F32))
        NBK = 6
        STATB = 7

        s_c = sem("s_c")
        s_hld = [sem(f"s_hld{i}") for i in range(2)]
        s_sq = sem("s_sq")
        s_stat = sem("s_stat")
        s_rs = sem("s_rs")
        s_nrm = sem("s_nrm")
        s_wld = [sem(f"s_wld{i}") for i in range(2)]
        s_wc = sem("s_wc")
        s_psd = [sem(f"s_psd{i}") for i in range(NBK)]
        s_oA = sem("s_oA")
        s_oD = sem("s_oD")
        s_stA = [sem(f"s_stA{i}") for i in range(2)]
        s_stD = [sem(f"s_stD{i}") for i in range(2)]
        s_rld = [sem(f"s_rld{i}") for i in range(2)]
        s_as = sem("s_as")
        s_ms = sem("s_ms")
        s_rc = sem("s_rc")

        tiles = []
        cntA = cntD = 0
        for gi, g in enumerate(groups):
            for tt in range(T // 512):
                if g["kind"] == "res":
                    tiles.append([g["blk"], gi, tt, "D", cntD]); cntD += 1
                else:
                    tiles.append([g["blk"], gi, tt, "A", cntA]); cntA += 1
        ntile = len(tiles)
        blk_last = {}
        for i, tl in enumerate(tiles):
            blk_last[tl[0]] = i

        def eng_wait_prev_bank(eng, i):
            if i >= NBK:
                p = tiles[i - NBK]
                eng.wait_ge(s_oA if p[3] == "A" else s_oD, p[4] + 1)

        with nc.Block() as block:
            @block.sync
            def _(sp):
                cnt = 0
                if prologue == "rmsnorm":
                    sp.dma_start(out=gsb[:], in_=dr["g"][:, :]).then_inc(s_c, 16); cnt += 16
                if has_bias:
                    sp.dma_start(out=bsb[:], in_=biasd[:, :]).then_inc(s_c, 16); cnt += 16
                if prologue == "plain":
                    xv = dr["xT"].rearrange("(c p) t -> p c t", p=128)
                    for q in range(4):
                        sp.dma_start(out=xn[:, :, q * 512:(q + 1) * 512],
                                     in_=xv[:, :, q * 512:(q + 1) * 512]).then_inc(s_hld[0], 16)
                else:
                    src = dr["hT"] if prologue == "rmsnorm" else dr["aT"]
                    sv = src.rearrange("(c p) t -> p c t", p=128)
                    if prologue == "mul":
                        bv = dr["bT"].rearrange("(c p) t -> p c t", p=128)
                    for tt in range(n1):
                        if tt >= 2:
                            sp.wait_ge(s_nrm, tt - 1)
                        sp.dma_start(out=hbuf[tt % 2][:], in_=sv[:, :, tt * TT1:(tt + 1) * TT1]).then_inc(s_hld[tt % 2], 16)
                        if prologue == "mul":
                            sp.dma_start(out=bbuf[tt % 2][:], in_=bv[:, :, tt * TT1:(tt + 1) * TT1]).then_inc(s_hld[tt % 2], 16)
                def wload(b):
                    w = min(WB, C - b * WB)
                    sp.dma_start(out=wst[b % 2][:, :, 0:w], in_=Wv[:, :, b * WB:b * WB + w]).then_inc(s_wld[b % 2], 16)
                wload(0)
                if nblk > 1:
                    wload(1)
                def rload(j):
                    tl = [t_ for t_ in tiles if t_[3] == "D" and t_[4] == j][0]
                    g = groups[tl[1]]
                    if j >= 2:
                        sp.wait_ge(s_oD, j - 1)
                    sp.dma_start(out=rbuf[j % 2][:], in_=rTd[g["row"]:g["row"] + 128, tl[2] * 512:(tl[2] + 1) * 512]).then_inc(s_rld[j % 2], 16)
                if has_res:
                    rload(0)
                cur_blk = 0
                for i, (blk, gi, tt, eng, j) in enumerate(tiles):
                    g = groups[gi]
                    if eng == "D" and j + 1 < cntD:
                        rload(j + 1)
                    if eng == "A":
                        sp.wait_ge(s_oA, j + 1)
                        src_t = obAh[j % 2] if g["dt"] == BF16 else obA[j % 2]
                        stsem = s_stA[j % 2]
                    else:
                        sp.wait_ge(s_oD, j + 1)
                        src_t = obD[j % 2]
                        stsem = s_stD[j % 2]
                    m = g["m"]
                    sp.dma_start(out=od[g["out"]][g["row"]:g["row"] + m, tt * 512:(tt + 1) * 512],
                                 in_=src_t[0:m, :]).then_inc(stsem, 16)
                    if i == blk_last[blk] and blk + 2 < nblk:
                        sp.wait_ge(s_wc, blk + 1)
                        wload(blk + 2)
                for k in range(2):
                    nA = len([1 for jj in range(cntA) if jj % 2 == k])
                    nD = len([1 for jj in range(cntD) if jj % 2 == k])
                    if nA:
                        sp.wait_ge(s_stA[k], 16 * nA)
                    if nD:
                        sp.wait_ge(s_stD[k], 16 * nD)

            @block.gpsimd
            def _(pool):
                if prologue == "rmsnorm":
                    pool.memset(ones[:], 1.0).then_inc(s_ms, 1)
                pool.memset(cst[:, 0:1], RMS_EPS).then_inc(s_ms, 1)
                pool.memset(cst[:, 1:2], 1.0).then_inc(s_ms, 1)
                for b in range(nblk):
                    w = min(WB, C - b * WB)
                    pool.wait_ge(s_wld[b % 2], 16 * (b // 2 + 1))
                    if b >= 2:
                        il = blk_last[b - 2]
                        pool.wait_ge(s_psd[il % NBK], il // NBK + 1)
                    pool.tensor_copy(out=wbf[b % 2][:, :, 0:w], in_=wst[b % 2][:, :, 0:w]).then_inc(s_wc, 1)

            @block.scalar
            def _(act):
                if has_bias:
                    act.wait_ge(s_c, 16 * ((1 if prologue == "rmsnorm" else 0) + 1))
                act.wait_ge(s_ms, 3 if prologue == "rmsnorm" else 2)
                if prologue == "rmsnorm":
                    for tt in range(n1):
                        act.wait_ge(s_hld[tt % 2], 16 * (tt // 2 + 1))
                        if tt >= 1:
                            act.wait_ge(s_stat, tt)
                        act.activation(out=sq[:], in_=hbuf[tt % 2][:], func=AF.Square).then_inc(s_sq, 1)
                        act.wait_ge(s_stat, tt + 1)
                        if tt >= 1:
                            act.wait_ge(s_rc, tt)
                        act.activation(out=rtmp[:], in_=ps[:, STATB, 0:TT1], func=AF.Sqrt, scale=1.0 / D,
                                       bias=cst[:, 0:1]).then_inc(s_rs, 1)
                nas = 0
                for i, (blk, gi, tt, eng, j) in enumerate(tiles):
                    if eng != "A":
                        continue
                    g = groups[gi]
                    m = g["m"]
                    bank = i % NBK
                    act.wait_ge(s_psd[bank], i // NBK + 1)
                    if j >= 2:
                        act.wait_ge(s_stA[j % 2], 16 * (j // 2))
                    pin = ps[0:m, bank, :]
                    if g["kind"] == "act":
                        dst = obAh[j % 2] if g["dt"] == BF16 else obA[j % 2]
                        kw = {}
                        if g.get("bias") is not None:
                            kw["bias"] = bsb[0:m, g["bias"]:g["bias"] + 1]
                        act.activation(out=dst[0:m, :], in_=pin, func=g["func"], **kw).then_inc(s_oA, 1)
                    else:
                        lt = ltmp[j % 2]
                        act.activation(out=lt[0:m, :], in_=pin, func=AF.Exp, scale=-1.0,
                                       bias=bsb[0:m, g["bias"]:g["bias"] + 1]).then_inc(s_as, 1); nas += 1
                        act.wait_ge(s_as, nas)
                        act.activation(out=lt[0:m, :], in_=lt[0:m, :], func=AF.Ln, bias=cst[0:m, 1:2]).then_inc(s_as, 1); nas += 1
                        act.wait_ge(s_as, nas)
                        act.mul(out=obA[j % 2][0:m, :], in_=lt[0:m, :], mul=-1.0).then_inc(s_oA, 1)

            @block.tensor
            def _(pe):
                if prologue == "rmsnorm":
                    pe.wait_ge(s_ms, 1)
                    for tt in range(n1):
                        pe.wait_ge(s_sq, tt + 1)
                        if tt >= 1:
                            pe.wait_ge(s_rs, tt)
                        for c in range(NCH):
                            mm = pe.matmul(ps[:, STATB, 0:TT1], lhsT=ones[:], rhs=sq[:, c, :], start=(c == 0), stop=(c == NCH - 1))
                        mm.then_inc(s_stat, 1)
                    pe.wait_ge(s_nrm, n1)
                elif prologue == "mul":
                    pe.wait_ge(s_nrm, n1)
                else:
                    pe.wait_ge(s_hld[0], 16 * 4)
                cur = -1
                for i, (blk, gi, tt, eng, j) in enumerate(tiles):
                    g = groups[gi]
                    m = g["m"]
                    if blk != cur:
                        pe.wait_ge(s_wc, blk + 1)
                        cur = blk
                    eng_wait_prev_bank(pe, i)
                    bank = i % NBK
                    co = g["col"] - blk * WB
                    for c in range(NCH):
                        mm = pe.matmul(ps[0:m, bank, :], lhsT=wbf[blk % 2][:, c, co:co + m],
                                       rhs=xn[:, c, tt * 512:(tt + 1) * 512], start=(c == 0), stop=(c == NCH - 1))
                    mm.then_inc(s_psd[bank], 1)

            @block.vector
            def _(dve):
                if prologue == "rmsnorm":
                    dve.wait_ge(s_c, 32 if has_bias else 16)
                    for tt in range(n1):
                        dve.wait_ge(s_rs, tt + 1)
                        dve.reciprocal(out=rstd[tt % 2][:], in_=rtmp[:]).then_inc(s_rc, 1)
                        dve.wait_ge(s_rc, tt + 1)
                        for c in range(NCH):
                            ins = dve.scalar_tensor_tensor(out=xn[:, c, tt * TT1:(tt + 1) * TT1], in0=hbuf[tt % 2][:, c, :],
                                                           scalar=gsb[:, c:c + 1], in1=rstd[tt % 2][:],
                                                           op0=ALU.mult, op1=ALU.mult)
                        ins.then_inc(s_nrm, 1)
                elif prologue == "mul":
                    for tt in range(n1):
                        dve.wait_ge(s_hld[tt % 2], 32 * (tt // 2 + 1))
                        dve.tensor_tensor(out=xn[:, :, tt * TT1:(tt + 1) * TT1], in0=hbuf[tt % 2][:], in1=bbuf[tt % 2][:],
                                          op=ALU.mult).then_inc(s_nrm, 1)
                for i, (blk, gi, tt, eng, j) in enumerate(tiles):
                    if eng != "D":
                        continue
                    bank = i % NBK
                    dve.wait_ge(s_psd[bank], i // NBK + 1)
                    dve.wait_ge(s_rld[j % 2], 16 * (j // 2 + 1))
                    if j >= 2:
                        dve.wait_ge(s_stD[j % 2], 16 * (j // 2))
                    dve.tensor_tensor(out=obD[j % 2][:], in0=ps[:, bank, :], in1=rbuf[j % 2][:], op=ALU.add).then_inc(s_oD, 1)
    return nc


def build_attn():
    nc = bass.Bass("TRN2", target_bir_lowering=False)
    es = contextlib.ExitStack()
    NH = 4
    NQT = S // 512
    NKB = S // 128
    scale = 128.0 ** -0.5
    qTd = nc.dram_tensor("qT", [NH, 128, S], BF16, kind="ExternalInput").ap()
    kTd = nc.dram_tensor("kT", [NH, 128, S], BF16, kind="ExternalInput").ap()
    vd = nc.dram_tensor("v", [NH, S, 128], BF16, kind="ExternalInput").ap()
    lfd = nc.dram_tensor("lf", [NH, S], F32, kind="ExternalInput").ap()
    maskd = nc.dram_tensor("mask", [128, 128], BF16, kind="ExternalInput").ap()
    oTd = nc.dram_tensor("oT", [NH * 128, S], F32, kind="ExternalOutput").ap()
    cscr = nc.dram_tensor("cscr", [NH, S], F32).ap()
    hiscr = nc.dram_tensor("hiscr", [NH, S], BF16).ap()
    loscr = nc.dram_tensor("loscr", [NH, S], BF16).ap()

    sb = lambda n, s, d: es.enter_context(nc.sbuf_tensor(n, s, d))
    sem = lambda n: es.enter_context(nc.semaphore(n))
    with es:
        lf4 = sb("lf4", [NH, S], F32)
        c4 = sb("c4", [NH, S], F32)
        d4 = sb("d4", [NH, S], F32)
        hi4 = sb("hi4", [NH, S], BF16)
        lo4 = sb("lo4", [NH, S], BF16)
        kt_sb = sb("kt_sb", [128, S], BF16)
        qt_sb = sb("qt_sb", [128, S], BF16)
        v_sb = sb("v_sb", [128, NKB, 128], BF16)
        ck_all = sb("ck_all", [128, NH, NKB], F32)
        r_rep = sb("r_rep", [128, NH, NQT], F32)
        bK = sb("bK", [128, NQT, NKB], F32)
        dq = [sb(f"dq{i}", [2, 512], BF16) for i in range(2)]
        NP = 12
        pt = [sb(f"pt{i}", [128, 512], BF16) for i in range(NP)]
        rec = sb("rec", [128, 512], F32)
        ob = [sb(f"ob{i}", [128, 512], F32) for i in range(2)]
        ones2 = sb("ones2", [2, 128], BF16)
        onesb = sb("onesb", [128, 128], BF16)
        mask = sb("mask_sb", [128, 128], BF16)
        ps = es.enter_context(nc.psum_tensor("ps", [128, 8, 512], F32))
        G = 2
        OB0 = 4
        RB0 = 6

        s_mk = sem("s_mk")
        s_in = sem("s_in"); s_p0 = sem("s_p0"); s_scr = sem("s_scr"); s_tab = sem("s_tab")
        s_hl = sem("s_hl"); s_bk = sem("s_bk")
        s_dq = [sem(f"s_dq{i}") for i in range(2)]
        s_s = sem("s_s"); s_exp = sem("s_exp"); s_mask = sem("s_mask")
        s_aD = sem("s_aD"); s_aP = sem("s_aP"); s_af = sem("s_af")
        s_rsum = sem("s_rsum"); s_ds = sem("s_ds"); s_on = sem("s_on")
        s_ost = [sem(f"s_ost{i}") for i in range(2)]
        s_ms = sem("s_ms")

        sched = []
        groups_h = []
        i = 0; md = 0; gg = 0
        for h in range(NH):
            hq = []; gh = []
            for qt in range(NQT):
                tl = []
                for kb in range(4 * qt + 4):
                    dg = kb >= 4 * qt
                    col0 = (kb - 4 * qt) * 128 if dg else 0
                    tl.append((i, kb, col0, dg, md if dg else -1))
                    i += 1
                    if dg:
                        md += 1
                hq.append(tl)
                for a in range(0, len(tl), G):
                    gh.append(dict(gg=gg, Q=h * NQT + qt, h=h, qt=qt, tiles=tl[a:a + G], first=(a == 0), last=(a + G >= len(tl))))
                    gg += 1
            sched.append(hq); groups_h.append(gh)
        NT = i
        flat_q = [tl for hq in sched for tl in hq]
        all_groups = [g for gh in groups_h for g in gh]
        last_group_of_q = {}
        for g in all_groups:
            last_group_of_q[g["Q"]] = g["gg"]

        aP_after = {}
        aD_after = {}

        def sbank(g, j):
            return (g["gg"] % 2) * G + j

        def cntD(i):
            return i // 2 + 1

        def cntP(i):
            return (i + 1) // 2

        def store(sp, Qs):
            h, qt = divmod(Qs, NQT)
            sp.wait_ge(s_on, Qs + 1)
            sp.dma_start(out=oTd[h * 128:(h + 1) * 128, qt * 512:(qt + 1) * 512], in_=ob[Qs % 2][:]).then_inc(s_ost[Qs % 2], 16)

        with nc.Block() as block:
            @block.sync
            def _(sp):
                sp.dma_start(out=lf4[:], in_=lfd[:, :]).then_inc(s_in, 16)
                sp.dma_start(out=mask[:], in_=maskd[:, :]).then_inc(s_mk, 16)
                sp.wait_ge(s_p0, 1)
                sp.dma_start(out=cscr[:, :], in_=c4[:]).then_inc(s_scr, 16)
                sp.wait_ge(s_p0, 2)
                sp.dma_start(out=hiscr[:, :], in_=hi4[:]).then_inc(s_scr, 16)
                sp.wait_ge(s_p0, 3)
                sp.dma_start(out=loscr[:, :], in_=lo4[:]).then_inc(s_scr, 16)
                sp.wait_ge(s_scr, 48)
                for h in range(NH):
                    sp.dma_start(out=ck_all[:, h, :], in_=cscr[h, :].rearrange("(n p) -> p n", p=128),
                                 allow_slow_non_contiguous=True).then_inc(s_tab, 16)
                    sp.dma_start(out=r_rep[:, h, :], in_=cscr[h, :].rearrange("(q w) -> q w", w=512)[:, 0].partition_broadcast(128),
                                 allow_slow_non_contiguous=True).then_inc(s_tab, 16)
                Q = 0
                for h in range(NH):
                    if h >= 1:
                        sp.wait_ge(s_rsum, h * NQT)
                    sp.dma_start(out=kt_sb[:], in_=kTd[h, :, :]).then_inc(s_hl, 16)
                    sp.dma_start(out=qt_sb[:], in_=qTd[h, :, :]).then_inc(s_hl, 16)
                    sp.dma_start(out=v_sb[:], in_=vd[h, :, :].rearrange("(n p) d -> p n d", p=128)).then_inc(s_hl, 16)
                    for qt in range(NQT):
                        if Q >= 2:
                            sp.wait_ge(s_s, last_group_of_q[Q - 2] + 1)
                        sp.dma_start(out=dq[Q % 2][0:1, :], in_=hiscr[h:h + 1, qt * 512:(qt + 1) * 512]).then_inc(s_dq[Q % 2], 16)
                        sp.dma_start(out=dq[Q % 2][1:2, :], in_=loscr[h:h + 1, qt * 512:(qt + 1) * 512]).then_inc(s_dq[Q % 2], 16)
                        if Q >= 1:
                            store(sp, Q - 1)
                        Q += 1
                store(sp, Q - 1)
                for k in range(2):
                    sp.wait_ge(s_ost[k], 16 * (NH * NQT // 2))

            @block.gpsimd
            def _(pool):
                pool.memset(ones2[:], 1.0).then_inc(s_ms, 1)
                pool.memset(onesb[:], 1.0).then_inc(s_ms, 1)
                pool.wait_ge(s_mk, 16)
                for tl in flat_q:
                    for (i, kb, col0, dg, m) in tl:
                        if dg:
                            pool.wait_ge(s_exp, i + 1)
                            pool.tensor_tensor(out=pt[i % NP][:, col0:col0 + 128], in0=pt[i % NP][:, col0:col0 + 128],
                                               in1=mask[:], op=ALU.mult).then_inc(s_mask, 1)

            @block.vector
            def _(dve):
                dve.wait_ge(s_in, 16)
                dve.tensor_tensor_scan(out=c4[:], data0=lf4[:], data1=lf4[:], initial=0.0, op0=ALU.add, op1=ALU.min).then_inc(s_p0, 1)
                dve.wait_ge(s_p0, 1)
                for qt in range(NQT):
                    sl = slice(qt * 512, (qt + 1) * 512)
                    ins = dve.tensor_scalar(out=d4[:, sl], in0=c4[:, sl], scalar1=c4[:, qt * 512:qt * 512 + 1], scalar2=1.0 / scale,
                                            op0=ALU.subtract, op1=ALU.mult)
                ins.then_inc(s_ds, 1)
                dve.wait_ge(s_ds, 1)
                dve.tensor_copy(out=hi4[:], in_=d4[:]).then_inc(s_p0, 1)
                dve.wait_ge(s_p0, 2)
                dve.tensor_copy(out=lf4[:], in_=hi4[:]).then_inc(s_ds, 1)
                dve.wait_ge(s_ds, 2)
                dve.tensor_tensor(out=d4[:], in0=d4[:], in1=lf4[:], op=ALU.subtract).then_inc(s_ds, 1)
                dve.wait_ge(s_ds, 3)
                dve.tensor_copy(out=lo4[:], in_=d4[:]).then_inc(s_p0, 1)
                dve.wait_ge(s_tab, 16 * 2 * NH)
                nds = 3
                nD = 0
                Q = 0
                for h in range(NH):
                    if h >= 1:
                        dve.wait_ge(s_exp, flat_q[h * NQT - 1][-1][0] + 1)
                    for qt in range(NQT):
                        nkb = 4 * qt + 4
                        ins = dve.tensor_scalar(out=bK[:, qt, 0:nkb], in0=ck_all[:, h, 0:nkb], scalar1=-1.0,
                                                scalar2=r_rep[:, h, qt:qt + 1], op0=ALU.mult, op1=ALU.add)
                    ins.then_inc(s_bk, 1)
                    for qt in range(NQT):
                        dve.wait_ge(s_rsum, Q + 1)
                        dve.reciprocal(out=rec[:], in_=ps[:, RB0 + Q % 2, :]).then_inc(s_ds, 1); nds += 1
                        dve.wait_ge(s_ds, nds)
                        if Q >= 2:
                            dve.wait_ge(s_ost[Q % 2], 16 * (Q // 2))
                        dve.tensor_tensor(out=ob[Q % 2][:], in0=ps[:, OB0 + Q % 2, :], in1=rec[:], op=ALU.mult).then_inc(s_on, 1)
                        Q += 1

            @block.scalar
            def _(act):
                for h in range(NH):
                    act.wait_ge(s_bk, h + 1)
                    for g in groups_h[h]:
                        act.wait_ge(s_s, g["gg"] + 1)
                        for j, (i, kb, col0, dg, m) in enumerate(g["tiles"]):
                            act.activation(out=pt[i % NP][:, col0:512], in_=ps[:, sbank(g, j), col0:512], func=AF.Exp,
                                           scale=scale, bias=bK[:, g["qt"], kb:kb + 1]).then_inc(s_exp, 1)

            @block.tensor
            def _(pe):
                pe.wait_ge(s_ms, 2)
                for h in range(NH):
                    pe.wait_ge(s_hl, 48 * (h + 1))
                    gh = groups_h[h]

                    def S_(g):
                        Q = g["Q"]; qt = g["qt"]
                        if g["first"]:
                            pe.wait_ge(s_dq[Q % 2], 32 * (Q // 2 + 1))
                        if g["gg"] >= 2:
                            pe.wait_ge(s_exp, all_groups[g["gg"] - 2]["tiles"][-1][0] + 1)
                        mm = None
                        for j, (i, kb, col0, dg, m) in enumerate(g["tiles"]):
                            bk = sbank(g, j)
                            pe.matmul(ps[:, bk, col0:512], lhsT=kt_sb[:, kb * 128:(kb + 1) * 128],
                                      rhs=qt_sb[:, qt * 512 + col0:(qt + 1) * 512], start=True, stop=False)
                            mm = pe.matmul(ps[:, bk, col0:512], lhsT=ones2[:], rhs=dq[Q % 2][:, col0:512],
                                           start=False, stop=True)
                        mm.then_inc(s_s, 1)

                    def PV_(g):
                        Q = g["Q"]
                        tl = g["tiles"]
                        pe.wait_ge(s_exp, tl[-1][0] + 1)
                        dms = [t[4] for t in tl if t[3]]
                        if dms:
                            pe.wait_ge(s_mask, max(dms) + 1)
                        if g["first"] and Q >= 2:
                            pe.wait_ge(s_on, Q - 1)
                        for j, (i, kb, col0, dg, m) in enumerate(tl):
                            pe.matmul(ps[:, OB0 + Q % 2, col0:512], lhsT=v_sb[:, kb, :], rhs=pt[i % NP][:, col0:512],
                                      start=(g["first"] and j == 0), stop=(g["last"] and j == len(tl) - 1))
                        mm = None
                        for j, (i, kb, col0, dg, m) in enumerate(tl):
                            mm = pe.matmul(ps[:, RB0 + Q % 2, col0:512], lhsT=onesb[:], rhs=pt[i % NP][:, col0:512],
                                           start=(g["first"] and j == 0), stop=(g["last"] and j == len(tl) - 1))
                        if g["last"]:
                            mm.then_inc(s_rsum, 1)

                    S_(gh[0])
                    if len(gh) > 1:
                        S_(gh[1])
                    for n, g in enumerate(gh):
                        PV_(g)
                        if n + 2 < len(gh):
                            S_(gh[n + 2])
    return nc


def build_conv():
    nc = bass.Bass("TRN2", target_bir_lowering=False)
    es = contextlib.ExitStack()
    TH = 1024
    NHALF = T // TH
    TE = TH + HALO
    KD = 16
    NADD = KCONV - KD - 1
    aTd = nc.dram_tensor("aT", [D, HALO + T], F32, kind="ExternalInput").ap()
    sbTd = nc.dram_tensor("sbT", [D, HALO + T], F32, kind="ExternalInput").ap()
    sgTd = nc.dram_tensor("sgT", [D, T], F32, kind="ExternalInput").ap()
    dwd = nc.dram_tensor("dw", [128, NCH, KCONV], F32, kind="ExternalInput").ap()
    dwbd = nc.dram_tensor("dwb", [128, NCH], F32, kind="ExternalInput").ap()
    lngd = nc.dram_tensor("lng", [128, NCH], F32, kind="ExternalInput").ap()
    lnbd = nc.dram_tensor("lnb", [128, NCH], F32, kind="ExternalInput").ap()
    yTd = nc.dram_tensor("yT", [D, T], BF16, kind="ExternalOutput").ap()
    aV = aTd.rearrange("(c p) t -> p c t", p=128)
    sbV = sbTd.rearrange("(c p) t -> p c t", p=128)
    sgV = sgTd.rearrange("(c p) t -> p c t", p=128)
    yV = yTd.rearrange("(c p) t -> p c t", p=128)

    sb = lambda n, s, d: es.enter_context(nc.sbuf_tensor(n, s, d))
    sem = lambda n: es.enter_context(nc.semaphore(n))
    with es:
        abuf = [sb(f"abuf{i}", [128, TE], F32) for i in range(2)]
        sbuf_ = [sb(f"sbuf{i}", [128, TE], F32) for i in range(2)]
        u = [sb(f"u{i}", [128, TE], F32) for i in range(2)]
        z_all = sb("z_all", [128, NCH, TH], F32)
        z2 = [sb(f"z2{i}", [128, TH], F32) for i in range(2)]
        tmp = [sb(f"tmp{i}", [128, TH], F32) for i in range(2)]
        zsq = sb("zsq", [128, TH], F32)
        mean = sb("mean", [128, TH], F32)
        msq = sb("msq", [128, TH], F32)
        rstd = sb("rstd", [128, TH], F32)
        sgb = [sb(f"sgb{i}", [128, TH], F32) for i in range(2)]
        sil = [sb(f"sil{i}", [128, TH], F32) for i in range(2)]
        yb = [sb(f"yb{i}", [128, TH], BF16) for i in range(2)]
        dw = sb("dw_sb", [128, NCH, KCONV], F32)
        dwb = sb("dwb_sb", [128, NCH], F32)
        lng = sb("lng_sb", [128, NCH], F32)
        lnb = sb("lnb_sb", [128, NCH], F32)
        onesf = sb("onesf", [128, 128], F32)
        cst = sb("cst", [128, 1], F32)
        ps = es.enter_context(nc.psum_tensor("ps", [128, 4, 512], F32))

        s_c = sem("s_c"); s_ms = sem("s_ms")
        s_ld = [sem(f"s_ld{i}") for i in range(2)]
        s_u = sem("s_u"); s_zd = sem("s_zd"); s_zp = sem("s_zp"); s_z = sem("s_z"); s_zsq = sem("s_zsq"); s_st = sem("s_st")
        s_dd = sem("s_dd"); s_pa = sem("s_pa"); s_t0 = sem("s_t0"); s_tm = sem("s_tm")
        s_stat = sem("s_stat"); s_sd = sem("s_sd"); s_n = sem("s_n"); s_sl = sem("s_sl"); s_y = sem("s_y")
        s_rcp = sem("s_rcp")
        s_sg = [sem(f"s_sg{i}") for i in range(2)]
        s_yst = [sem(f"s_yst{i}") for i in range(2)]

        NG = NHALF * NCH
        with nc.Block() as block:
            @block.sync
            def _(sp):
                sp.dma_start(out=dw[:], in_=dwd[:, :, :]).then_inc(s_c, 16)
                sp.dma_start(out=dwb[:], in_=dwbd[:, :]).then_inc(s_c, 16)
                sp.dma_start(out=lng[:], in_=lngd[:, :]).then_inc(s_c, 16)
                sp.dma_start(out=lnb[:], in_=lnbd[:, :]).then_inc(s_c, 16)

                def ld(G):
                    hf, c = divmod(G, NCH)
                    if G >= 2:
                        sp.wait_ge(s_u, G - 1)
                    sp.dma_start(out=abuf[G % 2][:], in_=aV[:, c, hf * TH:hf * TH + TE]).then_inc(s_ld[G % 2], 16)
                    sp.dma_start(out=sbuf_[G % 2][:], in_=sbV[:, c, hf * TH:hf * TH + TE]).then_inc(s_ld[G % 2], 16)

                def sgld(G):
                    hf, c = divmod(G, NCH)
                    if G >= 2:
                        sp.wait_ge(s_y, G - 1)
                    sp.dma_start(out=sgb[G % 2][:], in_=sgV[:, c, hf * TH:(hf + 1) * TH]).then_inc(s_sg[G % 2], 16)

                def yst(G):
                    hf, c = divmod(G, NCH)
                    sp.wait_ge(s_y, G + 1)
                    sp.dma_start(out=yV[:, c, hf * TH:(hf + 1) * TH], in_=yb[G % 2][:]).then_inc(s_yst[G % 2], 16)

                for hf in range(NHALF):
                    for c in range(NCH):
                        ld(hf * NCH + c)
                    for c in range(NCH):
                        G = hf * NCH + c
                        sgld(G)
                        if c >= 1:
                            yst(G - 1)
                    yst(hf * NCH + NCH - 1)
                for k in range(2):
                    sp.wait_ge(s_yst[k], 16 * (NG // 2))

            @block.gpsimd
            def _(pool):
                pool.memset(onesf[:], 1.0).then_inc(s_ms, 1)
                pool.memset(cst[:], LN_EPS).then_inc(s_ms, 1)
                pool.wait_ge(s_c, 64)

                def mk_u(G):
                    pool.wait_ge(s_ld[G % 2], 32 * (G // 2 + 1))
                    if G >= 2:
                        pool.wait_ge(s_zd, G - 1)
                        pool.wait_ge(s_t0, G - 1)
                        pool.wait_ge(s_tm, (G - 1) * NADD)
                    pool.tensor_tensor(out=u[G % 2][:], in0=abuf[G % 2][:], in1=sbuf_[G % 2][:], op=ALU.mult).then_inc(s_u, 1)

                for hf in range(NHALF):
                    mk_u(hf * NCH)
                    for c in range(NCH):
                        G = hf * NCH + c
                        if c + 1 < NCH:
                            mk_u(G + 1)
                        pool.wait_ge(s_t0, G + 1)
                        zz = z2[G % 2]
                        for n in range(NADD):
                            j = G * NADD + n
                            pool.wait_ge(s_tm, j + 1)
                            if j >= 1:
                                pool.wait_ge(s_pa, j)
                            pool.tensor_tensor(out=zz[:], in0=zz[:], in1=tmp[j % 2][:], op=ALU.add).then_inc(s_pa, 1)
                    for c in range(NCH):
                        G = hf * NCH + c
                        pool.wait_ge(s_sl, G + 1)
                        pool.wait_ge(s_sg[G % 2], 16 * (G // 2 + 1))
                        if G >= 2:
                            pool.wait_ge(s_yst[G % 2], 16 * (G // 2))
                        pool.tensor_tensor(out=yb[G % 2][:], in0=sil[G % 2][:], in1=sgb[G % 2][:], op=ALU.mult).then_inc(s_y, 1)

            @block.vector
            def _(dve):
                dve.wait_ge(s_c, 64)
                ndd = 0
                for hf in range(NHALF):
                    for c in range(NCH):
                        G = hf * NCH + c
                        dve.wait_ge(s_u, G + 1)
                        if hf >= 1:
                            dve.wait_ge(s_sl, (hf - 1) * NCH + c + 1)
                        zc = z_all[:, c, :]
                        for k in range(KD):
                            off = HALO - (KCONV - 1) + k
                            if k == 0:
                                ins = dve.tensor_scalar(out=zc, in0=u[G % 2][:, off:off + TH], scalar1=dw[:, c, 0:1],
                                                        scalar2=dwb[:, c:c + 1], op0=ALU.mult, op1=ALU.add)
                            else:
                                ins = dve.scalar_tensor_tensor(out=zc, in0=u[G % 2][:, off:off + TH], scalar=dw[:, c, k:k + 1],
                                                               in1=zc, op0=ALU.mult, op1=ALU.add)
                            if k < KD - 1:
                                ins.then_inc(s_dd, 1); ndd += 1
                                dve.wait_ge(s_dd, ndd)
                            else:
                                ins.then_inc(s_zd, 1)
                        dve.wait_ge(s_zd, G + 1)
                        dve.wait_ge(s_pa, (G + 1) * NADD)
                        dve.tensor_tensor(out=zc, in0=zc, in1=z2[G % 2][:], op=ALU.add).then_inc(s_z, 1)
                    dve.wait_ge(s_st, (hf + 1) * NCH)
                    if hf >= 1:
                        dve.wait_ge(s_n, hf * NCH)
                    psm = ps[:, 0:2, :].rearrange("p a b -> p (a b)")
                    psq = ps[:, 2:4, :].rearrange("p a b -> p (a b)")
                    dve.tensor_scalar(out=mean[:], in0=psm, scalar1=1.0 / D, scalar2=None, op0=ALU.mult).then_inc(s_dd, 1); ndd += 1
                    dve.wait_ge(s_dd, ndd)
                    dve.tensor_tensor(out=msq[:], in0=mean[:], in1=mean[:], op=ALU.mult).then_inc(s_dd, 1); ndd += 1
                    dve.wait_ge(s_dd, ndd)
                    dve.scalar_tensor_tensor(out=msq[:], in0=psq, scalar=1.0 / D, in1=msq[:], op0=ALU.mult, op1=ALU.subtract).then_inc(s_stat, 1)
                    dve.wait_ge(s_sd, hf + 1)
                    dve.reciprocal(out=rstd[:], in_=zsq[:]).then_inc(s_rcp, 1)
                    dve.wait_ge(s_rcp, hf + 1)
                    for c in range(NCH):
                        zc = z_all[:, c, :]
                        dve.tensor_tensor(out=zc, in0=zc, in1=mean[:], op=ALU.subtract).then_inc(s_dd, 1); ndd += 1
                        dve.wait_ge(s_dd, ndd)
                        dve.tensor_tensor(out=zc, in0=zc, in1=rstd[:], op=ALU.mult).then_inc(s_n, 1)

            @block.scalar
            def _(act):
                act.wait_ge(s_c, 64)
                act.wait_ge(s_ms, 2)
                def taps(G):
                    c = G % NCH
                    act.wait_ge(s_u, G + 1)
                    if G >= 2:
                        act.wait_ge(s_z, G - 1)
                    off = HALO - (KCONV - 1) + KD
                    act.activation(out=z2[G % 2][:], in_=u[G % 2][:, off:off + TH], func=AF.Copy,
                                   scale=dw[:, c, KD:KD + 1]).then_inc(s_t0, 1)
                    for n in range(NADD):
                        j = G * NADD + n
                        k = KD + 1 + n
                        off = HALO - (KCONV - 1) + k
                        if j >= 2:
                            act.wait_ge(s_pa, j - 1)
                        act.activation(out=tmp[j % 2][:], in_=u[G % 2][:, off:off + TH], func=AF.Copy,
                                       scale=dw[:, c, k:k + 1]).then_inc(s_tm, 1)

                for hf in range(NHALF):
                    taps(hf * NCH)
                    for c in range(NCH):
                        G = hf * NCH + c
                        if c + 1 < NCH:
                            taps(G + 1)
                        act.wait_ge(s_z, G + 1)
                        if G >= 1:
                            act.wait_ge(s_st, G)
                        if hf >= 1 and c == 0:
                            act.wait_ge(s_rcp, hf)
                        act.activation(out=zsq[:], in_=z_all[:, c, :], func=AF.Square).then_inc(s_zsq, 1)
                    act.wait_ge(s_stat, hf + 1)
                    act.wait_ge(s_st, (hf + 1) * NCH)
                    act.activation(out=zsq[:], in_=msq[:], func=AF.Sqrt, bias=cst[:, 0:1]).then_inc(s_sd, 1)
                    for c in range(NCH):
                        G = hf * NCH + c
                        act.wait_ge(s_n, G + 1)
                        if G >= 2:
                            act.wait_ge(s_y, G - 1)
                        act.activation(out=sil[G % 2][:], in_=z_all[:, c, :], func=AF.Silu, scale=lng[:, c:c + 1],
                                       bias=lnb[:, c:c + 1]).then_inc(s_sl, 1)

            @block.tensor
            def _(pe):
                pe.wait_ge(s_ms, 1)
                for hf in range(NHALF):
                    if hf >= 1:
                        pe.wait_ge(s_stat, hf)
                    for c in range(NCH):
                        G = hf * NCH + c
                        pe.wait_ge(s_z, G + 1)
                        for hh in range(2):
                            pe.matmul(ps[:, hh, :], lhsT=onesf[:], rhs=z_all[:, c, hh * 512:(hh + 1) * 512],
                                      start=(c == 0), stop=(c == NCH - 1))
                        pe.wait_ge(s_zsq, G + 1)
                        for hh in range(2):
                            mm = pe.matmul(ps[:, 2 + hh, :], lhsT=onesf[:], rhs=zsq[:, hh * 512:(hh + 1) * 512],
                                           start=(c == 0), stop=(c == NCH - 1))
                        mm.then_inc(s_st, 1)
    return nc


def build_norm():
    nc = bass.Bass("TRN2", target_bir_lowering=False)
    es = contextlib.ExitStack()
    TT1 = 256
    n1 = T // TT1
    hTd = nc.dram_tensor("hT", [D, T], F32, kind="ExternalInput").ap()
    gd = nc.dram_tensor("g", [128, NCH], F32, kind="ExternalInput").ap()
    oTd = nc.dram_tensor("outT", [D, T], F32, kind="ExternalOutput").ap()
    hV = hTd.rearrange("(c p) t -> p c t", p=128)
    oV = oTd.rearrange("(c p) t -> p c t", p=128)
    sb = lambda n, s, d: es.enter_context(nc.sbuf_tensor(n, s, d))
    sem = lambda n: es.enter_context(nc.semaphore(n))
    with es:
        hbuf = [sb(f"hbuf{i}", [128, NCH, TT1], F32) for i in range(2)]
        obuf = [sb(f"obuf{i}", [128, NCH, TT1], F32) for i in range(2)]
        sq = sb("sq", [128, NCH, TT1], F32)
        rtmp = sb("rtmp", [128, TT1], F32)
        rstd = sb("rstd", [128, TT1], F32)
        gsb = sb("gsb", [128, NCH], F32)
        ones = sb("ones", [128, 128], F32)
        cst = sb("cst", [128, 1], F32)
        ps = es.enter_context(nc.psum_tensor("ps", [128, 512], F32))
        s_c = sem("s_c"); s_ms = sem("s_ms"); s_sq = sem("s_sq"); s_stat = sem("s_stat"); s_rs = sem("s_rs")
        s_rc = sem("s_rc"); s_nrm = sem("s_nrm")
        s_hld = [sem(f"s_hld{i}") for i in range(2)]
        s_ost = [sem(f"s_ost{i}") for i in range(2)]
        with nc.Block() as block:
            @block.sync
            def _(sp):
                sp.dma_start(out=gsb[:], in_=gd[:, :]).then_inc(s_c, 16)
                for tt in range(n1):
                    if tt >= 2:
                        sp.wait_ge(s_nrm, tt - 1)
                    sp.dma_start(out=hbuf[tt % 2][:], in_=hV[:, :, tt * TT1:(tt + 1) * TT1]).then_inc(s_hld[tt % 2], 16)
                    if tt >= 1:
                        sp.wait_ge(s_nrm, tt)
                        sp.dma_start(out=oV[:, :, (tt - 1) * TT1:tt * TT1], in_=obuf[(tt - 1) % 2][:]).then_inc(s_ost[(tt - 1) % 2], 16)
                sp.wait_ge(s_nrm, n1)
                sp.dma_start(out=oV[:, :, (n1 - 1) * TT1:n1 * TT1], in_=obuf[(n1 - 1) % 2][:]).then_inc(s_ost[(n1 - 1) % 2], 16)
                for k in range(2):
                    sp.wait_ge(s_ost[k], 16 * (n1 // 2))

            @block.gpsimd
            def _(pool):
                pool.memset(ones[:], 1.0).then_inc(s_ms, 1)
                pool.memset(cst[:], RMS_EPS).then_inc(s_ms, 1)

            @block.scalar
            def _(act):
                act.wait_ge(s_ms, 2)
                for tt in range(n1):
                    act.wait_ge(s_hld[tt % 2], 16 * (tt // 2 + 1))
                    if tt >= 1:
                        act.wait_ge(s_stat, tt)
                    act.activation(out=sq[:], in_=hbuf[tt % 2][:], func=AF.Square).then_inc(s_sq, 1)
                    act.wait_ge(s_stat, tt + 1)
                    if tt >= 1:
                        act.wait_ge(s_rc, tt)
                    act.activation(out=rtmp[:], in_=ps[:, 0:TT1], func=AF.Sqrt, scale=1.0 / D, bias=cst[:, 0:1]).then_inc(s_rs, 1)

            @block.tensor
            def _(pe):
                pe.wait_ge(s_ms, 1)
                for tt in range(n1):
                    pe.wait_ge(s_sq, tt + 1)
                    if tt >= 1:
                        pe.wait_ge(s_rs, tt)
                    for c in range(NCH):
                        mm = pe.matmul(ps[:, 0:TT1], lhsT=ones[:], rhs=sq[:, c, :], start=(c == 0), stop=(c == NCH - 1))
                    mm.then_inc(s_stat, 1)

            @block.vector
            def _(dve):
                dve.wait_ge(s_c, 16)
                for tt in range(n1):
                    dve.wait_ge(s_rs, tt + 1)
                    if tt >= 1:
                        dve.wait_ge(s_nrm, tt)
                    dve.reciprocal(out=rstd[:], in_=rtmp[:]).then_inc(s_rc, 1)
                    dve.wait_ge(s_rc, tt + 1)
                    if tt >= 2:
                        dve.wait_ge(s_ost[tt % 2], 16 * (tt // 2))
                    for c in range(NCH):
                        ins = dve.scalar_tensor_tensor(out=obuf[tt % 2][:, c, :], in0=hbuf[tt % 2][:, c, :], scalar=gsb[:, c:c + 1],
                                                       in1=rstd[:], op0=ALU.mult, op1=ALU.mult)
                    ins.then_inc(s_nrm, 1)
    return nc


def _get(name, fn):
    if name not in _cache:
        _cache[name] = fn()
    return _cache[name]


def _fox_groups():
    gs = []
    names = ["qT", "kT", "vT", "sgT"]
    for gi in range(64):
        which = gi // 16
        gs.append(dict(kind="act", blk=(gi * 128) // 256, col=gi * 128, m=128, row=(gi % 16) * 128, out=names[which],
                       dt=(F32 if which == 3 else BF16), func=(AF.Silu if which == 3 else AF.Copy), bias=None))
    gs.append(dict(kind="lf", blk=32, col=8192, m=16, row=0, out="lfT", dt=F32, bias=64))
    outs = [("qT", D, BF16), ("kT", D, BF16), ("vT", D, BF16), ("sgT", D, F32), ("lfT", 16, F32)]
    return gs, outs


def _conv_groups():
    gs = []
    names = ["aT", "sbT", "sgT"]
    funcs = [AF.Identity, AF.Sigmoid, AF.Silu]
    for gi in range(48):
        which = gi // 16
        gs.append(dict(kind="act", blk=(gi * 128) // 256, col=gi * 128, m=128, row=(gi % 16) * 128, out=names[which],
                       dt=F32, func=funcs[which], bias=gi))
    outs = [("aT", D, F32), ("sbT", D, F32), ("sgT", D, F32)]
    return gs, outs


def _res_groups():
    gs = [dict(kind="res", blk=(gi * 128) // 256, col=gi * 128, m=128, row=gi * 128, out="hT_out", dt=F32) for gi in range(16)]
    return gs, [("hT_out", D, F32)]


def _pc(v):
    return np.ascontiguousarray(np.asarray(v, np.float32).reshape(NCH, 128).T)


def kernel(x, norm_g, fox_w_in, fox_b_f, fox_w_out, conv_w_in, conv_b_in, conv_dw,
           conv_dw_b, conv_ln_g, conv_ln_b, conv_w_out, final_norm_g):
    x = np.asarray(x, np.float32)
    B = x.shape[0]
    cores = [(b, t) for b in range(B) for t in range(4)]
    hT = [np.ascontiguousarray(x[b, t * T:(t + 1) * T, :].T) for (b, t) in cores]

    fg, fo = _fox_groups()
    cg, co = _conv_groups()
    rg, ro = _res_groups()
    k_fox_in = _get("fox_in", lambda: build_proj("rmsnorm", 8208, fg, fo, True))
    k_attn = _get("attn", build_attn)
    k_fox_out = _get("fox_out", lambda: build_proj("mul", D, rg, ro, False))
    k_conv_in = _get("conv_in", lambda: build_proj("rmsnorm", 3 * D, cg, co, True))
    k_conv = _get("conv", build_conv)
    k_conv_out = _get("conv_out", lambda: build_proj("plain", D, rg, ro, False))
    k_norm = _get("norm", build_norm)
    mask = np.triu(np.ones((128, 128), np.float32)).astype(ml_dtypes.bfloat16)

    for layer in range(4):
        j = layer // 2
        g2 = _pc(norm_g[layer])
        if layer % 2 == 0:
            W = np.ascontiguousarray(np.asarray(fox_w_in[j], np.float32))
            bias = np.zeros((128, 65), np.float32)
            bias[:16, 64] = -np.asarray(fox_b_f[j], np.float32)
            r1 = _run(k_fox_in, [{"hT": hT[c], "g": g2, "W": W, "bias": bias} for c in range(NCORES)])
            ims = []
            for (b, g) in cores:
                def cat(name, lo, hi):
                    return np.concatenate([np.asarray(r1[4 * b + t][name])[lo:hi, :] for t in range(4)], axis=1)
                qT = cat("qT", g * 512, (g + 1) * 512).reshape(4, 128, S)
                kT = cat("kT", g * 512, (g + 1) * 512).reshape(4, 128, S)
                vT = cat("vT", g * 512, (g + 1) * 512).reshape(4, 128, S)
                v = np.ascontiguousarray(vT.transpose(0, 2, 1))
                lf = np.ascontiguousarray(cat("lfT", 4 * g, 4 * g + 4))
                ims.append({"qT": np.ascontiguousarray(qT), "kT": np.ascontiguousarray(kT), "v": v, "lf": lf, "mask": mask})
            r2 = _run(k_attn, ims)
            Wo = np.ascontiguousarray(np.asarray(fox_w_out[j], np.float32))
            ims = []
            for ci, (b, t) in enumerate(cores):
                oT = np.concatenate([np.asarray(r2[4 * b + g]["oT"])[:, t * T:(t + 1) * T] for g in range(4)], axis=0)
                ims.append({"aT": np.ascontiguousarray(oT), "bT": np.asarray(r1[ci]["sgT"]), "W": Wo, "rT": hT[ci]})
            r3 = _run(k_fox_out, ims)
            hT = [np.asarray(r3[c]["hT_out"]) for c in range(NCORES)]
            if _dbg is not None:
                _dbg[f"o{layer}"] = [np.asarray(r2[c]["oT"]) for c in range(NCORES)]
        else:
            W = np.ascontiguousarray(np.asarray(conv_w_in[j], np.float32))
            bias = np.ascontiguousarray(np.asarray(conv_b_in[j], np.float32).reshape(48, 128).T)
            r1 = _run(k_conv_in, [{"hT": hT[c], "g": g2, "W": W, "bias": bias} for c in range(NCORES)])
            dw = np.ascontiguousarray(np.asarray(conv_dw[j], np.float32).T.reshape(NCH, 128, KCONV).transpose(1, 0, 2))
            ims = []
            for ci, (b, t) in enumerate(cores):
                def ext(name):
                    cur = np.asarray(r1[ci][name])
                    if t == 0:
                        halo = np.zeros((D, HALO), np.float32)
                    else:
                        halo = np.asarray(r1[ci - 1][name])[:, T - HALO:]
                    return np.ascontiguousarray(np.concatenate([halo, cur], axis=1))
                ims.append({"aT": ext("aT"), "sbT": ext("sbT"), "sgT": np.asarray(r1[ci]["sgT"]), "dw": dw,
                            "dwb": _pc(conv_dw_b[j]), "lng": _pc(conv_ln_g[j]), "lnb": _pc(conv_ln_b[j])})
            r2 = _run(k_conv, ims)
            Wo = np.ascontiguousarray(np.asarray(conv_w_out[j], np.float32))
            r3 = _run(k_conv_out, [{"xT": np.asarray(r2[c]["yT"]), "W": Wo, "rT": hT[c]} for c in range(NCORES)])
            hT = [np.asarray(r3[c]["hT_out"]) for c in range(NCORES)]
        if _dbg is not None:
            _dbg[f"h{layer}"] = [a.copy() for a in hT]
    r = _run(k_norm, [{"hT": hT[c], "g": _pc(final_norm_g)} for c in range(NCORES)])
    out = np.empty((B, S, D), np.float32)
    for ci, (b, t) in enumerate(cores):
        out[b, t * T:(t + 1) * T, :] = np.asarray(r[ci]["outT"]).T
    return out
```

```python
import contextlib
import numpy as np
import ml_dtypes
import concourse.bass as bass
import concourse.mybir as mybir
from concourse.bass_utils import run_bass_kernel_spmd

F32 = mybir.dt.float32
BF16 = mybir.dt.bfloat16
AF = mybir.ActivationFunctionType
ALU = mybir.AluOpType
AX = mybir.AxisListType

NCORES = 8
D = 2048
T = 2048
S = 8192
NCH = 16
RMS_EPS = 1e-6
LN_EPS = 1e-5
KCONV = 31
HALO = 32

_cache = {}
_dbg = None


def _run(nc, in_maps):
    res = run_bass_kernel_spmd(nc, in_maps, core_ids=list(range(NCORES)))
    return res.results


def build_proj(prologue, C, groups, outs, has_bias):
    nc = bass.Bass("TRN2", target_bir_lowering=False)
    es = contextlib.ExitStack()
    NG = len(groups)
    WB = 256
    nblk = (C + WB - 1) // WB
    dr = {}
    if prologue == "rmsnorm":
        dr["hT"] = nc.dram_tensor("hT", [D, T], F32, kind="ExternalInput").ap()
        dr["g"] = nc.dram_tensor("g", [128, NCH], F32, kind="ExternalInput").ap()
    elif prologue == "mul":
        dr["aT"] = nc.dram_tensor("aT", [D, T], F32, kind="ExternalInput").ap()
        dr["bT"] = nc.dram_tensor("bT", [D, T], F32, kind="ExternalInput").ap()
    else:
        dr["xT"] = nc.dram_tensor("xT", [D, T], BF16, kind="ExternalInput").ap()
    Wd = nc.dram_tensor("W", [D, C], F32, kind="ExternalInput").ap()
    Wv = Wd.rearrange("(c p) n -> p c n", p=128)
    if has_bias:
        biasd = nc.dram_tensor("bias", [128, NG], F32, kind="ExternalInput").ap()
    has_res = any(g["kind"] == "res" for g in groups)
    if has_res:
        rTd = nc.dram_tensor("rT", [C, T], F32, kind="ExternalInput").ap()
    od = {}
    for name, rows, dt in outs:
        od[name] = nc.dram_tensor(name, [rows, T], dt, kind="ExternalOutput").ap()

    sb = lambda n, s, d: es.enter_context(nc.sbuf_tensor(n, s, d))
    sem = lambda n: es.enter_context(nc.semaphore(n))
    with es:
        xn = sb("xn", [128, NCH, T], BF16)
        TT1 = 256
        n1 = T // TT1
        hbuf = [sb(f"hbuf{i}", [128, NCH, TT1], F32) for i in range(2)]
        if prologue == "rmsnorm":
            sq = sb("sq", [128, NCH, TT1], F32)
            rstd = [sb(f"rstd{i}", [128, TT1], F32) for i in range(2)]
            rtmp = sb("rtmp", [128, TT1], F32)
            gsb = sb("gsb", [128, NCH], F32)
            ones = sb("ones", [128, 128], F32)
        cst = sb("cst", [128, 2], F32)
        if prologue == "mul":
            bbuf = [sb(f"bbuf{i}", [128, NCH, TT1], F32) for i in range(2)]
        wst = [sb(f"wst{i}", [128, NCH, WB], F32) for i in range(2)]
        wbf = [sb(f"wbf{i}", [128, NCH, WB], BF16) for i in range(2)]
        obA = [sb(f"obA{i}", [128, 512], F32) for i in range(2)]
        obAh = [sb(f"obAh{i}", [128, 512], BF16) for i in range(2)]
        obD = [sb(f"obD{i}", [128, 512], F32) for i in range(2)]
        rbuf = [sb(f"rbuf{i}", [128, 512], F32) for i in range(2)]
        ltmp = [sb(f"ltmp{i}", [128, 512], F32) for i in range(2)]
        if has_bias:
            bsb = sb("bsb", [128, NG], F32)
        ps = es.enter_context(nc.psum_tensor("ps", [128, 8, 512], F32))
        NBK = 6
        STATB = 7

        s_c = sem("s_c")
        s_hld = [sem(f"s_hld{i}") for i in range(2)]
        s_sq = sem("s_sq")
        s_stat = sem("s_stat")
        s_rs = sem("s_rs")
        s_nrm = sem("s_nrm")
        s_wld = [sem(f"s_wld{i}") for i in range(2)]
        s_wc = sem("s_wc")
        s_psd = [sem(f"s_psd{i}") for i in range(NBK)]
        s_oA = sem("s_oA")
        s_oD = sem("s_oD")
        s_stA = [sem(f"s_stA{i}") for i in range(2)]
        s_stD = [sem(f"s_stD{i}") for i in range(2)]
        s_rld = [sem(f"s_rld{i}") for i in range(2)]
        s_as = sem("s_as")
        s_ms = sem("s_ms")
        s_rc = sem("s_rc")

        tiles = []
        cntA = cntD = 0
        for gi, g in enumerate(groups):
            for tt in range(T // 512):
                if g["kind"] == "res":
                    tiles.append([g["blk"], gi, tt, "D", cntD]); cntD += 1
                else:
                    tiles.append([g["blk"], gi, tt, "A", cntA]); cntA += 1
        ntile = len(tiles)
        blk_last = {}
        for i, tl in enumerate(tiles):
            blk_last[tl[0]] = i

        def eng_wait_prev_bank(eng, i):
            if i >= NBK:
                p = tiles[i - NBK]
                eng.wait_ge(s_oA if p[3] == "A" else s_oD, p[4] + 1)

        with nc.Block() as block:
            @block.sync
            def _(sp):
                cnt = 0
                if prologue == "rmsnorm":
                    sp.dma_start(out=gsb[:], in_=dr["g"][:, :]).then_inc(s_c, 16); cnt += 16
                if has_bias:
                    sp.dma_start(out=bsb[:], in_=biasd[:, :]).then_inc(s_c, 16); cnt += 16
                if prologue == "plain":
                    xv = dr["xT"].rearrange("(c p) t -> p c t", p=128)
                    for q in range(4):
                        sp.dma_start(out=xn[:, :, q * 512:(q + 1) * 512],
                                     in_=xv[:, :, q * 512:(q + 1) * 512]).then_inc(s_hld[0], 16)
                else:
                    src = dr["hT"] if prologue == "rmsnorm" else dr["aT"]
                    sv = src.rearrange("(c p) t -> p c t", p=128)
                    if prologue == "mul":
                        bv = dr["bT"].rearrange("(c p) t -> p c t", p=128)
                    for tt in range(n1):
                        if tt >= 2:
                            sp.wait_ge(s_nrm, tt - 1)
                        sp.dma_start(out=hbuf[tt % 2][:], in_=sv[:, :, tt * TT1:(tt + 1) * TT1]).then_inc(s_hld[tt % 2], 16)
                        if prologue == "mul":
                            sp.dma_start(out=bbuf[tt % 2][:], in_=bv[:, :, tt * TT1:(tt + 1) * TT1]).then_inc(s_hld[tt % 2], 16)
                def wload(b):
                    w = min(WB, C - b * WB)
                    sp.dma_start(out=wst[b % 2][:, :, 0:w], in_=Wv[:, :, b * WB:b * WB + w]).then_inc(s_wld[b % 2], 16)
                wload(0)
                if nblk > 1:
                    wload(1)
                def rload(j):
                    tl = [t_ for t_ in tiles if t_[3] == "D" and t_[4] == j][0]
                    g = groups[tl[1]]
                    if j >= 2:
                        sp.wait_ge(s_oD, j - 1)
                    sp.dma_start(out=rbuf[j % 2][:], in_=rTd[g["row"]:g["row"] + 128, tl[2] * 512:(tl[2] + 1) * 512]).then_inc(s_rld[j % 2], 16)
                if has_res:
                    rload(0)
                cur_blk = 0
                for i, (blk, gi, tt, eng, j) in enumerate(tiles):
                    g = groups[gi]
                    if eng == "D" and j + 1 < cntD:
                        rload(j + 1)
                    if eng == "A":
                        sp.wait_ge(s_oA, j + 1)
                        src_t = obAh[j % 2] if g["dt"] == BF16 else obA[j % 2]
                        stsem = s_stA[j % 2]
                    else:
                        sp.wait_ge(s_oD, j + 1)
                        src_t = obD[j % 2]
                        stsem = s_stD[j % 2]
                    m = g["m"]
                    sp.dma_start(out=od[g["out"]][g["row"]:g["row"] + m, tt * 512:(tt + 1) * 512],
                                 in_=src_t[0:m, :]).then_inc(stsem, 16)
                    if i == blk_last[blk] and blk + 2 < nblk:
                        sp.wait_ge(s_wc, blk + 1)
                        wload(blk + 2)
                for k in range(2):
                    nA = len([1 for jj in range(cntA) if jj % 2 == k])
                    nD = len([1 for jj in range(cntD) if jj % 2 == k])
                    if nA:
                        sp.wait_ge(s_stA[k], 16 * nA)
                    if nD:
                        sp.wait_ge(s_stD[k], 16 * nD)

            @block.gpsimd
            def _(pool):
                if prologue == "rmsnorm":
                    pool.memset(ones[:], 1.0).then_inc(s_ms, 1)
                pool.memset(cst[:, 0:1], RMS_EPS).then_inc(s_ms, 1)
                pool.memset(cst[:, 1:2], 1.0).then_inc(s_ms, 1)
                for b in range(nblk):
                    w = min(WB, C - b * WB)
                    pool.wait_ge(s_wld[b % 2], 16 * (b // 2 + 1))
                    if b >= 2:
                        il = blk_last[b - 2]
                        pool.wait_ge(s_psd[il % NBK], il // NBK + 1)
                    pool.tensor_copy(out=wbf[b % 2][:, :, 0:w], in_=wst[b % 2][:, :, 0:w]).then_inc(s_wc, 1)

            @block.scalar
            def _(act):
                if has_bias:
                    act.wait_ge(s_c, 16 * ((1 if prologue == "rmsnorm" else 0) + 1))
                act.wait_ge(s_ms, 3 if prologue == "rmsnorm" else 2)
                if prologue == "rmsnorm":
                    for tt in range(n1):
                        act.wait_ge(s_hld[tt % 2], 16 * (tt // 2 + 1))
                        if tt >= 1:
                            act.wait_ge(s_stat, tt)
                        act.activation(out=sq[:], in_=hbuf[tt % 2][:], func=AF.Square).then_inc(s_sq, 1)
                        act.wait_ge(s_stat, tt + 1)
                        if tt >= 1:
                            act.wait_ge(s_rc, tt)
                        act.activation(out=rtmp[:], in_=ps[:, STATB, 0:TT1], func=AF.Sqrt, scale=1.0 / D,
                                       bias=cst[:, 0:1]).then_inc(s_rs, 1)
                nas = 0
                for i, (blk, gi, tt, eng, j) in enumerate(tiles):
                    if eng != "A":
                        continue
                    g = groups[gi]
                    m = g["m"]
                    bank = i % NBK
                    act.wait_ge(s_psd[bank], i // NBK + 1)
                    if j >= 2:
                        act.wait_ge(s_stA[j % 2], 16 * (j // 2))
                    pin = ps[0:m, bank, :]
                    if g["kind"] == "act":
                        dst = obAh[j % 2] if g["dt"] == BF16 else obA[j % 2]
                        kw = {}
                        if g.get("bias") is not None:
                            kw["bias"] = bsb[0:m, g["bias"]:g["bias"] + 1]
                        act.activation(out=dst[0:m, :], in_=pin, func=g["func"], **kw).then_inc(s_oA, 1)
                    else:
                        lt = ltmp[j % 2]
                        act.activation(out=lt[0:m, :], in_=pin, func=AF.Exp, scale=-1.0,
                                       bias=bsb[0:m, g["bias"]:g["bias"] + 1]).then_inc(s_as, 1); nas += 1
                        act.wait_ge(s_as, nas)
                        act.activation(out=lt[0:m, :], in_=lt[0:m, :], func=AF.Ln, bias=cst[0:m, 1:2]).then_inc(s_as, 1); nas += 1
                        act.wait_ge(s_as, nas)
                        act.mul(out=obA[j % 2][0:m, :], in_=lt[0:m, :], mul=-1.0).then_inc(s_oA, 1)

            @block.tensor
            def _(pe):
                if prologue == "rmsnorm":
                    pe.wait_ge(s_ms, 1)
                    for tt in range(n1):
                        pe.wait_ge(s_sq, tt + 1)
                        if tt >= 1:
                            pe.wait_ge(s_rs, tt)
                        for c in range(NCH):
                            mm = pe.matmul(ps[:, STATB, 0:TT1], lhsT=ones[:], rhs=sq[:, c, :], start=(c == 0), stop=(c == NCH - 1))
                        mm.then_inc(s_stat, 1)
                    pe.wait_ge(s_nrm, n1)
                elif prologue == "mul":
                    pe.wait_ge(s_nrm, n1)
                else:
                    pe.wait_ge(s_hld[0], 16 * 4)
                cur = -1
                for i, (blk, gi, tt, eng, j) in enumerate(tiles):
                    g = groups[gi]
                    m = g["m"]
                    if blk != cur:
                        pe.wait_ge(s_wc, blk + 1)
                        cur = blk
                    eng_wait_prev_bank(pe, i)
                    bank = i % NBK
                    co = g["col"] - blk * WB
                    for c in range(NCH):
                        mm = pe.matmul(ps[0:m, bank, :], lhsT=wbf[blk % 2][:, c, co:co + m],
                                       rhs=xn[:, c, tt * 512:(tt + 1) * 512], start=(c == 0), stop=(c == NCH - 1))
                    mm.then_inc(s_psd[bank], 1)

            @block.vector
            def _(dve):
                if prologue == "rmsnorm":
                    dve.wait_ge(s_c, 32 if has_bias else 16)
                    for tt in range(n1):
                        dve.wait_ge(s_rs, tt + 1)
                        dve.reciprocal(out=rstd[tt % 2][:], in_=rtmp[:]).then_inc(s_rc, 1)
                        dve.wait_ge(s_rc, tt + 1)
                        for c in range(NCH):
                            ins = dve.scalar_tensor_tensor(out=xn[:, c, tt * TT1:(tt + 1) * TT1], in0=hbuf[tt % 2][:, c, :],
                                                           scalar=gsb[:, c:c + 1], in1=rstd[tt % 2][:],
                                                           op0=ALU.mult, op1=ALU.mult)
                        ins.then_inc(s_nrm, 1)
                elif prologue == "mul":
                    for tt in range(n1):
                        dve.wait_ge(s_hld[tt % 2], 32 * (tt // 2 + 1))
                        dve.tensor_tensor(out=xn[:, :, tt * TT1:(tt + 1) * TT1], in0=hbuf[tt % 2][:], in1=bbuf[tt % 2][:],
                                          op=ALU.mult).then_inc(s_nrm, 1)
                for i, (blk, gi, tt, eng, j) in enumerate(tiles):
                    if eng != "D":
                        continue
                    bank = i % NBK
                    dve.wait_ge(s_psd[bank], i // NBK + 1)
                    dve.wait_ge(s_rld[j % 2], 16 * (j // 2 + 1))
                    if j >= 2:
                        dve.wait_ge(s_stD[j % 2], 16 * (j // 2))
                    dve.tensor_tensor(out=obD[j % 2][:], in0=ps[:, bank, :], in1=rbuf[j % 2][:], op=ALU.add).then_inc(s_oD, 1)
    return nc


def build_attn():
    nc = bass.Bass("TRN2", target_bir_lowering=False)
    es = contextlib.ExitStack()
    NH = 4
    NQT = S // 512
    NKB = S // 128
    scale = 128.0 ** -0.5
    qTd = nc.dram_tensor("qT", [NH, 128, S], BF16, kind="ExternalInput").ap()
    kTd = nc.dram_tensor("kT", [NH, 128, S], BF16, kind="ExternalInput").ap()
    vd = nc.dram_tensor("v", [NH, S, 128], BF16, kind="ExternalInput").ap()
    lfd = nc.dram_tensor("lf", [NH, S], F32, kind="ExternalInput").ap()
    maskd = nc.dram_tensor("mask", [128, 128], BF16, kind="ExternalInput").ap()
    oTd = nc.dram_tensor("oT", [NH * 128, S], F32, kind="ExternalOutput").ap()
    cscr = nc.dram_tensor("cscr", [NH, S], F32).ap()
    hiscr = nc.dram_tensor("hiscr", [NH, S], BF16).ap()
    loscr = nc.dram_tensor("loscr", [NH, S], BF16).ap()

    sb = lambda n, s, d: es.enter_context(nc.sbuf_tensor(n, s, d))
    sem = lambda n: es.enter_context(nc.semaphore(n))
    with es:
        lf4 = sb("lf4", [NH, S], F32)
        c4 = sb("c4", [NH, S], F32)
        d4 = sb("d4", [NH, S], F32)
        hi4 = sb("hi4", [NH, S], BF16)
        lo4 = sb("lo4", [NH, S], BF16)
        kt_sb = sb("kt_sb", [128, S], BF16)
        qt_sb = sb("qt_sb", [128, S], BF16)
        v_sb = sb("v_sb", [128, NKB, 128], BF16)
        ck_all = sb("ck_all", [128, NH, NKB], F32)
        r_rep = sb("r_rep", [128, NH, NQT], F32)
        bK = sb("bK", [128, NQT, NKB], F32)
        dq = [sb(f"dq{i}", [2, 512], BF16) for i in range(2)]
        NP = 12
        pt = [sb(f"pt{i}", [128, 512], BF16) for i in range(NP)]
        rec = sb("rec", [128, 512], F32)
        ob = [sb(f"ob{i}", [128, 512], F32) for i in range(2)]
        ones2 = sb("ones2", [2, 128], BF16)
        onesb = sb("onesb", [128, 128], BF16)
        mask = sb("mask_sb", [128, 128], BF16)
        ps = es.enter_context(nc.psum_tensor("ps", [128, 8, 512], F32))
        G = 2
        OB0 = 4
        RB0 = 6

        s_mk = sem("s_mk")
        s_in = sem("s_in"); s_p0 = sem("s_p0"); s_scr = sem("s_scr"); s_tab = sem("s_tab")
        s_hl = sem("s_hl"); s_bk = sem("s_bk")
        s_dq = [sem(f"s_dq{i}") for i in range(2)]
        s_s = sem("s_s"); s_exp = sem("s_exp"); s_mask = sem("s_mask")
        s_aD = sem("s_aD"); s_aP = sem("s_aP"); s_af = sem("s_af")
        s_rsum = sem("s_rsum"); s_ds = sem("s_ds"); s_on = sem("s_on")
        s_ost = [sem(f"s_ost{i}") for i in range(2)]
        s_ms = sem("s_ms")

        sched = []
        groups_h = []
        i = 0; md = 0; gg = 0
        for h in range(NH):
            hq = []; gh = []
            for qt in range(NQT):
                tl = []
                for kb in range(4 * qt + 4):
                    dg = kb >= 4 * qt
                    col0 = (kb - 4 * qt) * 128 if dg else 0
                    tl.append((i, kb, col0, dg, md if dg else -1))
                    i += 1
                    if dg:
                        md += 1
                hq.append(tl)
                for a in range(0, len(tl), G):
                    gh.append(dict(gg=gg, Q=h * NQT + qt, h=h, qt=qt, tiles=tl[a:a + G], first=(a == 0), last=(a + G >= len(tl))))
                    gg += 1
            sched.append(hq); groups_h.append(gh)
        NT = i
        flat_q = [tl for hq in sched for tl in hq]
        all_groups = [g for gh in groups_h for g in gh]
        last_group_of_q = {}
        for g in all_groups:
            last_group_of_q[g["Q"]] = g["gg"]

        aP_after = {}
        aD_after = {}

        def sbank(g, j):
            return (g["gg"] % 2) * G + j

        def cntD(i):
            return i // 2 + 1

        def cntP(i):
            return (i + 1) // 2

        def store(sp, Qs):
            h, qt = divmod(Qs, NQT)
            sp.wait_ge(s_on, Qs + 1)
            sp.dma_start(out=oTd[h * 128:(h + 1) * 128, qt * 512:(qt + 1) * 512], in_=ob[Qs % 2][:]).then_inc(s_ost[Qs % 2], 16)

        with nc.Block() as block:
            @block.sync
            def _(sp):
                sp.dma_start(out=lf4[:], in_=lfd[:, :]).then_inc(s_in, 16)
                sp.dma_start(out=mask[:], in_=maskd[:, :]).then_inc(s_mk, 16)
                sp.wait_ge(s_p0, 1)
                sp.dma_start(out=cscr[:, :], in_=c4[:]).then_inc(s_scr, 16)
                sp.wait_ge(s_p0, 2)
                sp.dma_start(out=hiscr[:, :], in_=hi4[:]).then_inc(s_scr, 16)
                sp.wait_ge(s_p0, 3)
                sp.dma_start(out=loscr[:, :], in_=lo4[:]).then_inc(s_scr, 16)
                sp.wait_ge(s_scr, 48)
                for h in range(NH):
                    sp.dma_start(out=ck_all[:, h, :], in_=cscr[h, :].rearrange("(n p) -> p n", p=128),
                                 allow_slow_non_contiguous=True).then_inc(s_tab, 16)
                    sp.dma_start(out=r_rep[:, h, :], in_=cscr[h, :].rearrange("(q w) -> q w", w=512)[:, 0].partition_broadcast(128),
                                 allow_slow_non_contiguous=True).then_inc(s_tab, 16)
                Q = 0
                for h in range(NH):
                    if h >= 1:
                        sp.wait_ge(s_rsum, h * NQT)
                    sp.dma_start(out=kt_sb[:], in_=kTd[h, :, :]).then_inc(s_hl, 16)
                    sp.dma_start(out=qt_sb[:], in_=qTd[h, :, :]).then_inc(s_hl, 16)
                    sp.dma_start(out=v_sb[:], in_=vd[h, :, :].rearrange("(n p) d -> p n d", p=128)).then_inc(s_hl, 16)
                    for qt in range(NQT):
                        if Q >= 2:
                            sp.wait_ge(s_s, last_group_of_q[Q - 2] + 1)
                        sp.dma_start(out=dq[Q % 2][0:1, :], in_=hiscr[h:h + 1, qt * 512:(qt + 1) * 512]).then_inc(s_dq[Q % 2], 16)
                        sp.dma_start(out=dq[Q % 2][1:2, :], in_=loscr[h:h + 1, qt * 512:(qt + 1) * 512]).then_inc(s_dq[Q % 2], 16)
                        if Q >= 1:
                            store(sp, Q - 1)
                        Q += 1
                store(sp, Q - 1)
                for k in range(2):
                    sp.wait_ge(s_ost[k], 16 * (NH * NQT // 2))

            @block.gpsimd
            def _(pool):
                pool.memset(ones2[:], 1.0).then_inc(s_ms, 1)
                pool.memset(onesb[:], 1.0).then_inc(s_ms, 1)
                pool.wait_ge(s_mk, 16)
                for tl in flat_q:
                    for (i, kb, col0, dg, m) in tl:
                        if dg:
                            pool.wait_ge(s_exp, i + 1)
                            pool.tensor_tensor(out=pt[i % NP][:, col0:col0 + 128], in0=pt[i % NP][:, col0:col0 + 128],
                                               in1=mask[:], op=ALU.mult).then_inc(s_mask, 1)

            @block.vector
            def _(dve):
                dve.wait_ge(s_in, 16)
                dve.tensor_tensor_scan(out=c4[:], data0=lf4[:], data1=lf4[:], initial=0.0, op0=ALU.add, op1=ALU.min).then_inc(s_p0, 1)
                dve.wait_ge(s_p0, 1)
                for qt in range(NQT):
                    sl = slice(qt * 512, (qt + 1) * 512)
                    ins = dve.tensor_scalar(out=d4[:, sl], in0=c4[:, sl], scalar1=c4[:, qt * 512:qt * 512 + 1], scalar2=1.0 / scale,
                                            op0=ALU.subtract, op1=ALU.mult)
                ins.then_inc(s_ds, 1)
                dve.wait_ge(s_ds, 1)
                dve.tensor_copy(out=hi4[:], in_=d4[:]).then_inc(s_p0, 1)
                dve.wait_ge(s_p0, 2)
                dve.tensor_copy(out=lf4[:], in_=hi4[:]).then_inc(s_ds, 1)
                dve.wait_ge(s_ds, 2)
                dve.tensor_tensor(out=d4[:], in0=d4[:], in1=lf4[:], op=ALU.subtract).then_inc(s_ds, 1)
                dve.wait_ge(s_ds, 3)
                dve.tensor_copy(out=lo4[:], in_=d4[:]).then_inc(s_p0, 1)
                dve.wait_ge(s_tab, 16 * 2 * NH)
                nds = 3
                nD = 0
                Q = 0
                for h in range(NH):
                    if h >= 1:
                        dve.wait_ge(s_exp, flat_q[h * NQT - 1][-1][0] + 1)
                    for qt in range(NQT):
                        nkb = 4 * qt + 4
                        ins = dve.tensor_scalar(out=bK[:, qt, 0:nkb], in0=ck_all[:, h, 0:nkb], scalar1=-1.0,
                                                scalar2=r_rep[:, h, qt:qt + 1], op0=ALU.mult, op1=ALU.add)
                    ins.then_inc(s_bk, 1)
                    for qt in range(NQT):
                        dve.wait_ge(s_rsum, Q + 1)
                        dve.reciprocal(out=rec[:], in_=ps[:, RB0 + Q % 2, :]).then_inc(s_ds, 1); nds += 1
                        dve.wait_ge(s_ds, nds)
                        if Q >= 2:
                            dve.wait_ge(s_ost[Q % 2], 16 * (Q // 2))
                        dve.tensor_tensor(out=ob[Q % 2][:], in0=ps[:, OB0 + Q % 2, :], in1=rec[:], op=ALU.mult).then_inc(s_on, 1)
                        Q += 1

            @block.scalar
            def _(act):
                for h in range(NH):
                    act.wait_ge(s_bk, h + 1)
                    for g in groups_h[h]:
                        act.wait_ge(s_s, g["gg"] + 1)
                        for j, (i, kb, col0, dg, m) in enumerate(g["tiles"]):
                            act.activation(out=pt[i % NP][:, col0:512], in_=ps[:, sbank(g, j), col0:512], func=AF.Exp,
                                           scale=scale, bias=bK[:, g["qt"], kb:kb + 1]).then_inc(s_exp, 1)

            @block.tensor
            def _(pe):
                pe.wait_ge(s_ms, 2)
                for h in range(NH):
                    pe.wait_ge(s_hl, 48 * (h + 1))
                    gh = groups_h[h]

                    def S_(g):
                        Q = g["Q"]; qt = g["qt"]
                        if g["first"]:
                            pe.wait_ge(s_dq[Q % 2], 32 * (Q // 2 + 1))
                        if g["gg"] >= 2:
                            pe.wait_ge(s_exp, all_groups[g["gg"] - 2]["tiles"][-1][0] + 1)
                        mm = None
                        for j, (i, kb, col0, dg, m) in enumerate(g["tiles"]):
                            bk = sbank(g, j)
                            pe.matmul(ps[:, bk, col0:512], lhsT=kt_sb[:, kb * 128:(kb + 1) * 128],
                                      rhs=qt_sb[:, qt * 512 + col0:(qt + 1) * 512], start=True, stop=False)
                            mm = pe.matmul(ps[:, bk, col0:512], lhsT=ones2[:], rhs=dq[Q % 2][:, col0:512],
                                           start=False, stop=True)
                        mm.then_inc(s_s, 1)

                    def PV_(g):
                        Q = g["Q"]
                        tl = g["tiles"]
                        pe.wait_ge(s_exp, tl[-1][0] + 1)
                        dms = [t[4] for t in tl if t[3]]
                        if dms:
                            pe.wait_ge(s_mask, max(dms) + 1)
                        if g["first"] and Q >= 2:
                            pe.wait_ge(s_on, Q - 1)
                        for j, (i, kb, col0, dg, m) in enumerate(tl):
                            pe.matmul(ps[:, OB0 + Q % 2, col0:512], lhsT=v_sb[:, kb, :], rhs=pt[i % NP][:, col0:512],
                                      start=(g["first"] and j == 0), stop=(g["last"] and j == len(tl) - 1))
                        mm = None
                        for j, (i, kb, col0, dg, m) in enumerate(tl):
                            mm = pe.matmul(ps[:, RB0 + Q % 2, col0:512], lhsT=onesb[:], rhs=pt[i % NP][:, col0:512],
                                           start=(g["first"] and j == 0), stop=(g["last"] and j == len(tl) - 1))
                        if g["last"]:
                            mm.then_inc(s_rsum, 1)

                    S_(gh[0])
                    if len(gh) > 1:
                        S_(gh[1])
                    for n, g in enumerate(gh):
                        PV_(g)
                        if n + 2 < len(gh):
                            S_(gh[n + 2])
    return nc


def build_conv():
    nc = bass.Bass("TRN2", target_bir_lowering=False)
    es = contextlib.ExitStack()
    TH = 1024
    NHALF = T // TH
    TE = TH + HALO
    aTd = nc.dram_tensor("aT", [D, HALO + T], F32, kind="ExternalInput").ap()
    sbTd = nc.dram_tensor("sbT", [D, HALO + T], F32, kind="ExternalInput").ap()
    sgTd = nc.dram_tensor("sgT", [D, T], F32, kind="ExternalInput").ap()
    dwd = nc.dram_tensor("dw", [128, NCH, KCONV], F32, kind="ExternalInput").ap()
    dwbd = nc.dram_tensor("dwb", [128, NCH], F32, kind="ExternalInput").ap()
    lngd = nc.dram_tensor("lng", [128, NCH], F32, kind="ExternalInput").ap()
    lnbd = nc.dram_tensor("lnb", [128, NCH], F32, kind="ExternalInput").ap()
    identd = nc.dram_tensor("ident", [128, 128], BF16, kind="ExternalInput").ap()
    yTd = nc.dram_tensor("yT", [D, T], BF16, kind="ExternalOutput").ap()
    aV = aTd.rearrange("(c p) t -> p c t", p=128)
    sbV = sbTd.rearrange("(c p) t -> p c t", p=128)
    sgV = sgTd.rearrange("(c p) t -> p c t", p=128)
    yV = yTd.rearrange("(c p) t -> p c t", p=128)

    sb = lambda n, s, d: es.enter_context(nc.sbuf_tensor(n, s, d))
    sem = lambda n: es.enter_context(nc.semaphore(n))
    with es:
        abuf = [sb(f"abuf{i}", [128, TE], F32) for i in range(2)]
        sbuf_ = [sb(f"sbuf{i}", [128, TE], F32) for i in range(2)]
        ub = [sb(f"ub{i}", [128, TE], BF16) for i in range(2)]
        dg = [sb(f"dg{i}", [128, KCONV, 128], BF16) for i in range(2)]
        z_all = sb("z_all", [128, NCH, TH], F32)
        zsq = sb("zsq", [128, TH], F32)
        mean = sb("mean", [128, TH], F32)
        msq = sb("msq", [128, TH], F32)
        rstd = sb("rstd", [128, TH], F32)
        sgb = [sb(f"sgb{i}", [128, TH], F32) for i in range(2)]
        sil = [sb(f"sil{i}", [128, TH], F32) for i in range(2)]
        yb = [sb(f"yb{i}", [128, TH], BF16) for i in range(2)]
        dw = sb("dw_sb", [128, NCH, KCONV], F32)
        dwb = sb("dwb_sb", [128, NCH], F32)
        lng = sb("lng_sb", [128, NCH], F32)
        lnb = sb("lnb_sb", [128, NCH], F32)
        ident = sb("ident_sb", [128, 128], BF16)
        onesf = sb("onesf", [128, 128], F32)
        cst = sb("cst", [128, 1], F32)
        ps = es.enter_context(nc.psum_tensor("ps", [128, 8, 512], F32))
        CB0 = 4

        s_c = sem("s_c"); s_ms = sem("s_ms")
        s_ld = [sem(f"s_ld{i}") for i in range(2)]
        s_u = sem("s_u"); s_dg = sem("s_dg"); s_cv = sem("s_cv"); s_ev = sem("s_ev"); s_zsq = sem("s_zsq"); s_st = sem("s_st")
        s_dd = sem("s_dd")
        s_stat = sem("s_stat"); s_sd = sem("s_sd"); s_n = sem("s_n"); s_sl = sem("s_sl"); s_y = sem("s_y")
        s_rcp = sem("s_rcp")
        s_sg = [sem(f"s_sg{i}") for i in range(2)]
        s_yst = [sem(f"s_yst{i}") for i in range(2)]

        NG = NHALF * NCH
        with nc.Block() as block:
            @block.sync
            def _(sp):
                sp.dma_start(out=dw[:], in_=dwd[:, :, :]).then_inc(s_c, 16)
                sp.dma_start(out=dwb[:], in_=dwbd[:, :]).then_inc(s_c, 16)
                sp.dma_start(out=lng[:], in_=lngd[:, :]).then_inc(s_c, 16)
                sp.dma_start(out=lnb[:], in_=lnbd[:, :]).then_inc(s_c, 16)
                sp.dma_start(out=ident[:], in_=identd[:, :]).then_inc(s_c, 16)

                def ld(G):
                    hf, c = divmod(G, NCH)
                    if G >= 2:
                        sp.wait_ge(s_u, G - 1)
                    sp.dma_start(out=abuf[G % 2][:], in_=aV[:, c, hf * TH:hf * TH + TE]).then_inc(s_ld[G % 2], 16)
                    sp.dma_start(out=sbuf_[G % 2][:], in_=sbV[:, c, hf * TH:hf * TH + TE]).then_inc(s_ld[G % 2], 16)

                def sgld(G):
                    hf, c = divmod(G, NCH)
                    if G >= 2:
                        sp.wait_ge(s_y, G - 1)
                    sp.dma_start(out=sgb[G % 2][:], in_=sgV[:, c, hf * TH:(hf + 1) * TH]).then_inc(s_sg[G % 2], 16)

                def yst(G):
                    hf, c = divmod(G, NCH)
                    sp.wait_ge(s_y, G + 1)
                    sp.dma_start(out=yV[:, c, hf * TH:(hf + 1) * TH], in_=yb[G % 2][:]).then_inc(s_yst[G % 2], 16)

                for hf in range(NHALF):
                    for c in range(NCH):
                        ld(hf * NCH + c)
                    for c in range(NCH):
                        G = hf * NCH + c
                        sgld(G)
                        if c >= 1:
                            yst(G - 1)
                    yst(hf * NCH + NCH - 1)
                for k in range(2):
                    sp.wait_ge(s_yst[k], 16 * (NG // 2))

            @block.gpsimd
            def _(pool):
                pool.memset(onesf[:], 1.0).then_inc(s_ms, 1)
                pool.memset(cst[:], LN_EPS).then_inc(s_ms, 1)
                pool.wait_ge(s_c, 80)
                for hf in range(NHALF):
                    for c in range(NCH):
                        G = hf * NCH + c
                        pool.wait_ge(s_ld[G % 2], 32 * (G // 2 + 1))
                        if G >= 2:
                            pool.wait_ge(s_cv, 2 * (G - 1))
                        pool.tensor_tensor(out=ub[G % 2][:], in0=abuf[G % 2][:], in1=sbuf_[G % 2][:], op=ALU.mult).then_inc(s_u, 1)
                    for c in range(NCH):
                        G = hf * NCH + c
                        pool.wait_ge(s_sl, G + 1)
                        pool.wait_ge(s_sg[G % 2], 16 * (G // 2 + 1))
                        if G >= 2:
                            pool.wait_ge(s_yst[G % 2], 16 * (G // 2))
                        pool.tensor_tensor(out=yb[G % 2][:], in0=sil[G % 2][:], in1=sgb[G % 2][:], op=ALU.mult).then_inc(s_y, 1)

            @block.vector
            def _(dve):
                dve.wait_ge(s_c, 80)
                ndd = 0
                for hf in range(NHALF):
                    for c in range(NCH):
                        G = hf * NCH + c
                        if hf >= 1:
                            dve.wait_ge(s_sl, (hf - 1) * NCH + c + 1)
                        for tt in range(2):
                            n = 2 * G + tt
                            dve.wait_ge(s_cv, n + 1)
                            dve.tensor_scalar(out=z_all[:, c, tt * 512:(tt + 1) * 512], in0=ps[:, CB0 + n % 4, :],
                                              scalar1=dwb[:, c:c + 1], scalar2=None, op0=ALU.add).then_inc(s_ev, 1)
                    dve.wait_ge(s_st, (hf + 1) * NCH)
                    psm = ps[:, 0:2, :].rearrange("p a b -> p (a b)")
                    psq = ps[:, 2:4, :].rearrange("p a b -> p (a b)")
                    dve.tensor_scalar(out=mean[:], in0=psm, scalar1=1.0 / D, scalar2=None, op0=ALU.mult).then_inc(s_dd, 1); ndd += 1
                    dve.wait_ge(s_dd, ndd)
                    dve.tensor_tensor(out=msq[:], in0=mean[:], in1=mean[:], op=ALU.mult).then_inc(s_dd, 1); ndd += 1
                    dve.wait_ge(s_dd, ndd)
                    dve.scalar_tensor_tensor(out=msq[:], in0=psq, scalar=1.0 / D, in1=msq[:], op0=ALU.mult, op1=ALU.subtract).then_inc(s_stat, 1)
                    dve.wait_ge(s_sd, hf + 1)
                    dve.reciprocal(out=rstd[:], in_=zsq[:]).then_inc(s_rcp, 1)
                    dve.wait_ge(s_rcp, hf + 1)
                    for c in range(NCH):
                        zc = z_all[:, c, :]
                        dve.tensor_tensor(out=zc, in0=zc, in1=mean[:], op=ALU.subtract).then_inc(s_dd, 1); ndd += 1
                        dve.wait_ge(s_dd, ndd)
                        dve.tensor_tensor(out=zc, in0=zc, in1=rstd[:], op=ALU.mult).then_inc(s_n, 1)

            @block.scalar
            def _(act):
                act.wait_ge(s_c, 80)
                act.wait_ge(s_ms, 2)

                def mk_dg(G):
                    c = G % NCH
                    if G >= 2:
                        act.wait_ge(s_cv, 2 * (G - 1))
                    ins = None
                    for k in range(KCONV):
                        ins = act.activation(out=dg[G % 2][:, k, :], in_=ident[:], func=AF.Copy, scale=dw[:, c, k:k + 1])
                    ins.then_inc(s_dg, 1)

                for hf in range(NHALF):
                    mk_dg(hf * NCH)
                    for c in range(NCH):
                        G = hf * NCH + c
                        if c + 1 < NCH:
                            mk_dg(G + 1)
                        act.wait_ge(s_ev, 2 * (G + 1))
                        if G >= 1:
                            act.wait_ge(s_st, G)
                        if hf >= 1 and c == 0:
                            act.wait_ge(s_rcp, hf)
                        act.activation(out=zsq[:], in_=z_all[:, c, :], func=AF.Square).then_inc(s_zsq, 1)
                    act.wait_ge(s_stat, hf + 1)
                    act.wait_ge(s_st, (hf + 1) * NCH)
                    act.activation(out=zsq[:], in_=msq[:], func=AF.Sqrt, bias=cst[:, 0:1]).then_inc(s_sd, 1)
                    for c in range(NCH):
                        G = hf * NCH + c
                        act.wait_ge(s_n, G + 1)
                        if G >= 2:
                            act.wait_ge(s_y, G - 1)
                        act.activation(out=sil[G % 2][:], in_=z_all[:, c, :], func=AF.Silu, scale=lng[:, c:c + 1],
                                       bias=lnb[:, c:c + 1]).then_inc(s_sl, 1)

            @block.tensor
            def _(pe):
                pe.wait_ge(s_ms, 1)

                def conv_mm(G):
                    pe.wait_ge(s_u, G + 1)
                    pe.wait_ge(s_dg, G + 1)
                    for tt in range(2):
                        n = 2 * G + tt
                        if n >= 4:
                            pe.wait_ge(s_ev, n - 3)
                        mm = None
                        for k in range(KCONV):
                            off = HALO - (KCONV - 1) + k + tt * 512
                            mm = pe.matmul(ps[:, CB0 + n % 4, :], lhsT=dg[G % 2][:, k, :], rhs=ub[G % 2][:, off:off + 512],
                                           start=(k == 0), stop=(k == KCONV - 1))
                        mm.then_inc(s_cv, 1)

                def stats(G, hf, c):
                    if c == 0 and hf >= 1:
                        pe.wait_ge(s_stat, hf)
                    pe.wait_ge(s_ev, 2 * (G + 1))
                    for hh in range(2):
                        pe.matmul(ps[:, hh, :], lhsT=onesf[:], rhs=z_all[:, c, hh * 512:(hh + 1) * 512],
                                  start=(c == 0), stop=(c == NCH - 1))
                    pe.wait_ge(s_zsq, G + 1)
                    mm = None
                    for hh in range(2):
                        mm = pe.matmul(ps[:, 2 + hh, :], lhsT=onesf[:], rhs=zsq[:, hh * 512:(hh + 1) * 512],
                                       start=(c == 0), stop=(c == NCH - 1))
                    mm.then_inc(s_st, 1)

                for hf in range(NHALF):
                    conv_mm(hf * NCH)
                    for c in range(NCH):
                        G = hf * NCH + c
                        if c + 1 < NCH:
                            conv_mm(G + 1)
                        stats(G, hf, c)
    return nc


def build_norm():
    nc = bass.Bass("TRN2", target_bir_lowering=False)
    es = contextlib.ExitStack()
    TT1 = 256
    n1 = T // TT1
    hTd = nc.dram_tensor("hT", [D, T], F32, kind="ExternalInput").ap()
    gd = nc.dram_tensor("g", [128, NCH], F32, kind="ExternalInput").ap()
    oTd = nc.dram_tensor("outT", [D, T], F32, kind="ExternalOutput").ap()
    hV = hTd.rearrange("(c p) t -> p c t", p=128)
    oV = oTd.rearrange("(c p) t -> p c t", p=128)
    sb = lambda n, s, d: es.enter_context(nc.sbuf_tensor(n, s, d))
    sem = lambda n: es.enter_context(nc.semaphore(n))
    with es:
        hbuf = [sb(f"hbuf{i}", [128, NCH, TT1], F32) for i in range(2)]
        obuf = [sb(f"obuf{i}", [128, NCH, TT1], F32) for i in range(2)]
        sq = sb("sq", [128, NCH, TT1], F32)
        rtmp = sb("rtmp", [128, TT1], F32)
        rstd = sb("rstd", [128, TT1], F32)
        gsb = sb("gsb", [128, NCH], F32)
        ones = sb("ones", [128, 128], F32)
        cst = sb("cst", [128, 1], F32)
        ps = es.enter_context(nc.psum_tensor("ps", [128, 512], F32))
        s_c = sem("s_c"); s_ms = sem("s_ms"); s_sq = sem("s_sq"); s_stat = sem("s_stat"); s_rs = sem("s_rs")
        s_rc = sem("s_rc"); s_nrm = sem("s_nrm")
        s_hld = [sem(f"s_hld{i}") for i in range(2)]
        s_ost = [sem(f"s_ost{i}") for i in range(2)]
        with nc.Block() as block:
            @block.sync
            def _(sp):
                sp.dma_start(out=gsb[:], in_=gd[:, :]).then_inc(s_c, 16)
                for tt in range(n1):
                    if tt >= 2:
                        sp.wait_ge(s_nrm, tt - 1)
                    sp.dma_start(out=hbuf[tt % 2][:], in_=hV[:, :, tt * TT1:(tt + 1) * TT1]).then_inc(s_hld[tt % 2], 16)
                    if tt >= 1:
                        sp.wait_ge(s_nrm, tt)
                        sp.dma_start(out=oV[:, :, (tt - 1) * TT1:tt * TT1], in_=obuf[(tt - 1) % 2][:]).then_inc(s_ost[(tt - 1) % 2], 16)
                sp.wait_ge(s_nrm, n1)
                sp.dma_start(out=oV[:, :, (n1 - 1) * TT1:n1 * TT1], in_=obuf[(n1 - 1) % 2][:]).then_inc(s_ost[(n1 - 1) % 2], 16)
                for k in range(2):
                    sp.wait_ge(s_ost[k], 16 * (n1 // 2))

            @block.gpsimd
            def _(pool):
                pool.memset(ones[:], 1.0).then_inc(s_ms, 1)
                pool.memset(cst[:], RMS_EPS).then_inc(s_ms, 1)

            @block.scalar
            def _(act):
                act.wait_ge(s_ms, 2)
                for tt in range(n1):
                    act.wait_ge(s_hld[tt % 2], 16 * (tt // 2 + 1))
                    if tt >= 1:
                        act.wait_ge(s_stat, tt)
                    act.activation(out=sq[:], in_=hbuf[tt % 2][:], func=AF.Square).then_inc(s_sq, 1)
                    act.wait_ge(s_stat, tt + 1)
                    if tt >= 1:
                        act.wait_ge(s_rc, tt)
                    act.activation(out=rtmp[:], in_=ps[:, 0:TT1], func=AF.Sqrt, scale=1.0 / D, bias=cst[:, 0:1]).then_inc(s_rs, 1)

            @block.tensor
            def _(pe):
                pe.wait_ge(s_ms, 1)
                for tt in range(n1):
                    pe.wait_ge(s_sq, tt + 1)
                    if tt >= 1:
                        pe.wait_ge(s_rs, tt)
                    for c in range(NCH):
                        mm = pe.matmul(ps[:, 0:TT1], lhsT=ones[:], rhs=sq[:, c, :], start=(c == 0), stop=(c == NCH - 1))
                    mm.then_inc(s_stat, 1)

            @block.vector
            def _(dve):
                dve.wait_ge(s_c, 16)
                for tt in range(n1):
                    dve.wait_ge(s_rs, tt + 1)
                    if tt >= 1:
                        dve.wait_ge(s_nrm, tt)
                    dve.reciprocal(out=rstd[:], in_=rtmp[:]).then_inc(s_rc, 1)
                    dve.wait_ge(s_rc, tt + 1)
                    if tt >= 2:
                        dve.wait_ge(s_ost[tt % 2], 16 * (tt // 2))
                    for c in range(NCH):
                        ins = dve.scalar_tensor_tensor(out=obuf[tt % 2][:, c, :], in0=hbuf[tt % 2][:, c, :], scalar=gsb[:, c:c + 1],
                                                       in1=rstd[:], op0=ALU.mult, op1=ALU.mult)
                    ins.then_inc(s_nrm, 1)
    return nc


def _get(name, fn):
    if name not in _cache:
        _cache[name] = fn()
    return _cache[name]


def _fox_groups():
    gs = []
    names = ["qT", "kT", "vT", "sgT"]
    for gi in range(64):
        which = gi // 16
        gs.append(dict(kind="act", blk=(gi * 128) // 256, col=gi * 128, m=128, row=(gi % 16) * 128, out=names[which],
                       dt=(F32 if which == 3 else BF16), func=(AF.Silu if which == 3 else AF.Copy), bias=None))
    gs.append(dict(kind="lf", blk=32, col=8192, m=16, row=0, out="lfT", dt=F32, bias=64))
    outs = [("qT", D, BF16), ("kT", D, BF16), ("vT", D, BF16), ("sgT", D, F32), ("lfT", 16, F32)]
    return gs, outs


def _conv_groups():
    gs = []
    names = ["aT", "sbT", "sgT"]
    funcs = [AF.Identity, AF.Sigmoid, AF.Silu]
    for gi in range(48):
        which = gi // 16
        gs.append(dict(kind="act", blk=(gi * 128) // 256, col=gi * 128, m=128, row=(gi % 16) * 128, out=names[which],
                       dt=F32, func=funcs[which], bias=gi))
    outs = [("aT", D, F32), ("sbT", D, F32), ("sgT", D, F32)]
    return gs, outs


def _res_groups():
    gs = [dict(kind="res", blk=(gi * 128) // 256, col=gi * 128, m=128, row=gi * 128, out="hT_out", dt=F32) for gi in range(16)]
    return gs, [("hT_out", D, F32)]


def _pc(v):
    return np.ascontiguousarray(np.asarray(v, np.float32).reshape(NCH, 128).T)


def kernel(x, norm_g, fox_w_in, fox_b_f, fox_w_out, conv_w_in, conv_b_in, conv_dw,
           conv_dw_b, conv_ln_g, conv_ln_b, conv_w_out, final_norm_g):
    x = np.asarray(x, np.float32)
    B = x.shape[0]
    cores = [(b, t) for b in range(B) for t in range(4)]
    hT = [np.ascontiguousarray(x[b, t * T:(t + 1) * T, :].T) for (b, t) in cores]

    fg, fo = _fox_groups()
    cg, co = _conv_groups()
    rg, ro = _res_groups()
    k_fox_in = _get("fox_in", lambda: build_proj("rmsnorm", 8208, fg, fo, True))
    k_attn = _get("attn", build_attn)
    k_fox_out = _get("fox_out", lambda: build_proj("mul", D, rg, ro, False))
    k_conv_in = _get("conv_in", lambda: build_proj("rmsnorm", 3 * D, cg, co, True))
    k_conv = _get("conv", build_conv)
    k_conv_out = _get("conv_out", lambda: build_proj("plain", D, rg, ro, False))
    k_norm = _get("norm", build_norm)
    mask = np.triu(np.ones((128, 128), np.float32)).astype(ml_dtypes.bfloat16)

    for layer in range(4):
        j = layer // 2
        g2 = _pc(norm_g[layer])
        if layer % 2 == 0:
            W = np.ascontiguousarray(np.asarray(fox_w_in[j], np.float32))
            bias = np.zeros((128, 65), np.float32)
            bias[:16, 64] = -np.asarray(fox_b_f[j], np.float32)
            r1 = _run(k_fox_in, [{"hT": hT[c], "g": g2, "W": W, "bias": bias} for c in range(NCORES)])
            ims = []
            for (b, g) in cores:
                def cat(name, lo, hi):
                    return np.concatenate([np.asarray(r1[4 * b + t][name])[lo:hi, :] for t in range(4)], axis=1)
                qT = cat("qT", g * 512, (g + 1) * 512).reshape(4, 128, S)
                kT = cat("kT", g * 512, (g + 1) * 512).reshape(4, 128, S)
                vT = cat("vT", g * 512, (g + 1) * 512).reshape(4, 128, S)
                v = np.ascontiguousarray(vT.transpose(0, 2, 1))
                lf = np.ascontiguousarray(cat("lfT", 4 * g, 4 * g + 4))
                ims.append({"qT": np.ascontiguousarray(qT), "kT": np.ascontiguousarray(kT), "v": v, "lf": lf, "mask": mask})
            r2 = _run(k_attn, ims)
            Wo = np.ascontiguousarray(np.asarray(fox_w_out[j], np.float32))
            ims = []
            for ci, (b, t) in enumerate(cores):
                oT = np.concatenate([np.asarray(r2[4 * b + g]["oT"])[:, t * T:(t + 1) * T] for g in range(4)], axis=0)
                ims.append({"aT": np.ascontiguousarray(oT), "bT": np.asarray(r1[ci]["sgT"]), "W": Wo, "rT": hT[ci]})
            r3 = _run(k_fox_out, ims)
            hT = [np.asarray(r3[c]["hT_out"]) for c in range(NCORES)]
            if _dbg is not None:
                _dbg[f"o{layer}"] = [np.asarray(r2[c]["oT"]) for c in range(NCORES)]
        else:
            W = np.ascontiguousarray(np.asarray(conv_w_in[j], np.float32))
            bias = np.ascontiguousarray(np.asarray(conv_b_in[j], np.float32).reshape(48, 128).T)
            r1 = _run(k_conv_in, [{"hT": hT[c], "g": g2, "W": W, "bias": bias} for c in range(NCORES)])
            dw = np.ascontiguousarray(np.asarray(conv_dw[j], np.float32).T.reshape(NCH, 128, KCONV).transpose(1, 0, 2))
            ims = []
            for ci, (b, t) in enumerate(cores):
                def ext(name):
                    cur = np.asarray(r1[ci][name])
                    if t == 0:
                        halo = np.zeros((D, HALO), np.float32)
                    else:
                        halo = np.asarray(r1[ci - 1][name])[:, T - HALO:]
                    return np.ascontiguousarray(np.concatenate([halo, cur], axis=1))
                ims.append({"aT": ext("aT"), "sbT": ext("sbT"), "sgT": np.asarray(r1[ci]["sgT"]), "dw": dw,
                            "ident": np.eye(128, dtype=np.float32).astype(ml_dtypes.bfloat16),
                            "dwb": _pc(conv_dw_b[j]), "lng": _pc(conv_ln_g[j]), "lnb": _pc(conv_ln_b[j])})
            r2 = _run(k_conv, ims)
            Wo = np.ascontiguousarray(np.asarray(conv_w_out[j], np.float32))
            r3 = _run(k_conv_out, [{"xT": np.asarray(r2[c]["yT"]), "W": Wo, "rT": hT[c]} for c in range(NCORES)])
            hT = [np.asarray(r3[c]["hT_out"]) for c in range(NCORES)]
        if _dbg is not None:
            _dbg[f"h{layer}"] = [a.copy() for a in hT]
    r = _run(k_norm, [{"hT": hT[c], "g": _pc(final_norm_g)} for c in range(NCORES)])
    out = np.empty((B, S, D), np.float32)
    for ci, (b, t) in enumerate(cores):
        out[b, t * T:(t + 1) * T, :] = np.asarray(r[ci]["outT"]).T
    return out
```
